# Optimizing a Trainium2 kernel written in Bass

```python
import jax
import jax.numpy as jnp
from jax import lax
import numpy as np

D_MODEL = 1024
BATCH = 32
SEQ = 2048
DEPTH = 2

CHUNK = 64
D_PLE = 256
N_EVEN = (DEPTH + 1) // 2
N_ODD = DEPTH // 2
D_FF = 2816
A_HEADS = 8
A_KV_HEADS = 2
A_GROUP = A_HEADS // A_KV_HEADS
A_HEAD_DIM = 64
A_WIDTH = A_HEADS * A_HEAD_DIM
A_KV_WIDTH = A_KV_HEADS * A_HEAD_DIM
A_WINDOW = 128
A_PREV_CHUNKS = A_WINDOW // CHUNK
B_WIDTH = 512
B_BLOCKS = 8
B_BLOCK = B_WIDTH // B_BLOCKS
B_CONV = 4
RG_C = 8.0
AB_PROJ = A_WIDTH + 2 * A_KV_WIDTH + 2 * B_WIDTH
C_HEADS = 8
C_HEAD_DIM = 128
C_WIDTH = C_HEADS * C_HEAD_DIM
C_CONV = 4
C_PROJ = 4 * C_WIDTH + 2 * C_HEADS
DN_ALPHA = (2.0 * DEPTH) ** 0.25
DN_BETA = (8.0 * DEPTH) ** -0.25
LN_EPS = 1e-5
NORM_EPS = 1e-6
NEG = -1e30

kernel_name = 'hybrid_swa_rglru_gdn_deepnorm_macaron'


def layer_norm(x, g, b):
    xf = x.astype(jnp.float32)
    mu = jnp.mean(xf, -1, keepdims=True)
    var = jnp.mean(jnp.square(xf - mu), -1, keepdims=True)
    return ((xf - mu) * lax.rsqrt(var + LN_EPS) * g + b).astype(x.dtype)


def swiglu(x, wg, wu, wd):
    return (jax.nn.silu(x @ wg) * (x @ wu)) @ wd


def causal_dwconv(x, w):
    k, s = w.shape[0], x.shape[1]
    xp = jnp.pad(x, ((0, 0), (k - 1, 0), (0, 0)))
    y = xp[:, 0:s] * w[0]
    for j in range(1, k):
        y = y + xp[:, j:j + s] * w[j]
    return y


def chunk_band(t, n_prev):
    b, s = t.shape[:2]
    nc = s // CHUNK
    pad = n_prev * CHUNK
    tp = jnp.pad(t, ((0, 0), (pad, 0), (0, 0), (0, 0)))
    return jnp.concatenate(
        [tp[:, j * CHUNK:j * CHUNK + s].reshape(b, nc, CHUNK, *t.shape[2:]) for j in range(n_prev + 1)],
        axis=2)


def alibi_slopes(n):
    return 2.0 ** (-8.0 * jnp.arange(1, n + 1, dtype=jnp.float32) / n)


def sliding_window_sink_attention(q, k, v, sinks):
    b, s = q.shape[:2]
    nc = s // CHUNK
    pad = A_PREV_CHUNKS * CHUNK
    nk = pad + CHUNK
    qb = q.reshape(b, nc, CHUNK, A_KV_HEADS, A_GROUP, A_HEAD_DIM)
    kb = chunk_band(k, A_PREV_CHUNKS)
    vb = chunk_band(v, A_PREV_CHUNKS)
    sc = jnp.einsum('bnckgd,bnskd->bnkgcs', qb, kb).astype(jnp.float32) * (A_HEAD_DIM ** -0.5)
    dist = jnp.abs(jnp.arange(CHUNK)[:, None] + pad - jnp.arange(nk)[None, :]).astype(jnp.float32)
    slopes = alibi_slopes(A_HEADS).reshape(A_KV_HEADS, A_GROUP)
    valid = (jnp.arange(nc)[:, None] * CHUNK + jnp.arange(nk)[None, :] - pad) >= 0
    sc = sc - slopes[:, :, None, None] * dist
    sc = jnp.where(valid[None, :, None, None, None, :], sc, NEG)
    sink = sinks.astype(jnp.float32).reshape(A_KV_HEADS, A_GROUP)[:, :, None]
    m = jnp.maximum(sc.max(-1), sink)
    pr = jnp.exp(sc - m[..., None])
    den = pr.sum(-1) + jnp.exp(sink - m)
    o = jnp.einsum('bnkgcs,bnskd->bnckgd', pr / den[..., None], vb.astype(jnp.float32))
    return o.reshape(b, s, A_WIDTH).astype(q.dtype)


def rg_lru(x, w_a, b_a, w_x, b_x, lam):
    xb = x.reshape(*x.shape[:2], B_BLOCKS, B_BLOCK)
    r = jax.nn.sigmoid(jnp.einsum('bshi,hij->bshj', xb, w_a).reshape(x.shape) + b_a)
    i = jax.nn.sigmoid(jnp.einsum('bshi,hij->bshj', xb, w_x).reshape(x.shape) + b_x)
    log_a = (-RG_C * r * jax.nn.softplus(-lam)).astype(jnp.float32)
    a = jnp.exp(log_a)
    u = jnp.sqrt(-jnp.expm1(2.0 * log_a)) * (i * x).astype(jnp.float32)

    def combine(c1, c2):
        a1, b1 = c1
        a2, b2 = c2
        return a1 * a2, a2 * b1 + b2

    _, h = lax.associative_scan(combine, (a, u), axis=1)
    return h.astype(x.dtype)


def mixer_ab(x, w_in, sinks, conv_w, conv_b, w_a, b_a, w_x, b_x, lam, w_out):
    b, s = x.shape[:2]
    proj = x @ w_in
    o1 = A_WIDTH
    o2 = o1 + A_KV_WIDTH
    o3 = o2 + A_KV_WIDTH
    o4 = o3 + B_WIDTH
    q = proj[..., :o1].reshape(b, s, A_HEADS, A_HEAD_DIM)
    k = proj[..., o1:o2].reshape(b, s, A_KV_HEADS, A_HEAD_DIM)
    v = proj[..., o2:o3].reshape(b, s, A_KV_HEADS, A_HEAD_DIM)
    bx = proj[..., o3:o4]
    bg = proj[..., o4:]
    ya = sliding_window_sink_attention(q, k, v, sinks)
    bx = causal_dwconv(bx, conv_w) + conv_b
    yb = rg_lru(bx, w_a, b_a, w_x, b_x, lam) * jax.nn.gelu(bg)
    return jnp.concatenate([ya, yb], axis=-1) @ w_out


def gated_delta_rule(q, k, v, g, beta):
    f32 = jnp.float32
    b, s, h, dk = q.shape
    dv = v.shape[-1]
    nc = s // CHUNK

    def to_chunks(t):
        return t.astype(f32).reshape(b, nc, CHUNK, h, -1).transpose(1, 0, 3, 2, 4)

    q = to_chunks(q) * (dk ** -0.5)
    k = to_chunks(k)
    v = to_chunks(v)
    g = g.astype(f32).reshape(b, nc, CHUNK, h).transpose(1, 0, 3, 2)
    beta = beta.astype(f32).reshape(b, nc, CHUNK, h).transpose(1, 0, 3, 2)
    gc = jnp.cumsum(g, axis=-1)
    tril = jnp.tril(jnp.ones((CHUNK, CHUNK), bool))
    strict = jnp.tril(jnp.ones((CHUNK, CHUNK), bool), -1)
    diff = gc[..., :, None] - gc[..., None, :]
    decay = jnp.where(tril, jnp.exp(jnp.where(tril, diff, 0.0)), 0.0)
    kb = k * beta[..., None]
    lmat = jnp.where(strict, jnp.einsum('nbhid,nbhjd->nbhij', kb, k) * decay, 0.0)
    amat = lmat + jnp.eye(CHUNK, dtype=f32)
    u = lax.linalg.triangular_solve(amat, v * beta[..., None], left_side=True, lower=True, unit_diagonal=True)
    w = lax.linalg.triangular_solve(amat, kb * jnp.exp(gc)[..., None], left_side=True, lower=True, unit_diagonal=True)
    attn = jnp.einsum('nbhid,nbhjd->nbhij', q, k) * decay
    qg = q * jnp.exp(gc)[..., None]
    kdec = k * jnp.exp(gc[..., -1:] - gc)[..., None]
    glast = jnp.exp(gc[..., -1])

    def step(state, xs):
        qg_n, kdec_n, w_n, u_n, attn_n, gl_n = xs
        v_new = u_n - jnp.einsum('bhcd,bhde->bhce', w_n, state)
        o = jnp.einsum('bhcd,bhde->bhce', qg_n, state) + jnp.einsum('bhij,bhje->bhie', attn_n, v_new)
        state = state * gl_n[..., None, None] + jnp.einsum('bhcd,bhce->bhde', kdec_n, v_new)
        return state, o

    s0 = jnp.zeros((b, h, dk, dv), f32)
    _, o = lax.scan(step, s0, (qg, kdec, w, u, attn, glast))
    return o.transpose(1, 0, 3, 2, 4).reshape(b, s, h, dv)


def mixer_c(x, w_in, conv_w, a_log, dt_bias, norm_g, w_out):
    b, s = x.shape[:2]
    proj = x @ w_in
    qkv = jax.nn.silu(causal_dwconv(proj[..., :3 * C_WIDTH], conv_w))
    z = proj[..., 3 * C_WIDTH:4 * C_WIDTH].reshape(b, s, C_HEADS, C_HEAD_DIM)
    b_logit = proj[..., 4 * C_WIDTH:4 * C_WIDTH + C_HEADS]
    a_in = proj[..., 4 * C_WIDTH + C_HEADS:]
    q = qkv[..., :C_WIDTH].reshape(b, s, C_HEADS, C_HEAD_DIM).astype(jnp.float32)
    k = qkv[..., C_WIDTH:2 * C_WIDTH].reshape(b, s, C_HEADS, C_HEAD_DIM).astype(jnp.float32)
    v = qkv[..., 2 * C_WIDTH:].reshape(b, s, C_HEADS, C_HEAD_DIM)
    q = q * lax.rsqrt(jnp.sum(q * q, -1, keepdims=True) + NORM_EPS)
    k = k * lax.rsqrt(jnp.sum(k * k, -1, keepdims=True) + NORM_EPS)
    beta = jax.nn.sigmoid(b_logit.astype(jnp.float32))
    g = -jnp.exp(a_log.astype(jnp.float32)) * jax.nn.softplus((a_in + dt_bias).astype(jnp.float32))
    o = gated_delta_rule(q, k, v, g, beta)
    o = o * lax.rsqrt(jnp.mean(o * o, -1, keepdims=True) + NORM_EPS) * norm_g
    o = (o * jax.nn.silu(z.astype(jnp.float32))).astype(x.dtype)
    return o.reshape(b, s, C_WIDTH) @ w_out


def setup_inputs(seed: int = 0) -> dict:
    key = jax.random.key(seed)
    ks = iter(jax.random.split(key, 48))
    f32 = jnp.float32

    def nrm(shape, scale):
        return jax.random.normal(next(ks), shape, f32) * scale

    d = D_MODEL
    x = nrm((BATCH, SEQ, d), 1.0)
    p = nrm((DEPTH, BATCH, SEQ, D_PLE), 1.0)
    ffn1_wg = nrm((DEPTH, d, D_FF), d ** -0.5)
    ffn1_wu = nrm((DEPTH, d, D_FF), d ** -0.5)
    ffn1_wd = nrm((DEPTH, D_FF, d), DN_BETA * D_FF ** -0.5)
    ffn2_wg = nrm((DEPTH, d, D_FF), d ** -0.5)
    ffn2_wu = nrm((DEPTH, d, D_FF), d ** -0.5)
    ffn2_wd = nrm((DEPTH, D_FF, d), DN_BETA * D_FF ** -0.5)
    ln_g = 1.0 + nrm((DEPTH, 3, d), 0.02)
    ln_b = nrm((DEPTH, 3, d), 0.02)
    ple_wg = nrm((DEPTH, d, d), d ** -0.5)
    ple_bg = nrm((DEPTH, d), 0.02)
    ple_wp = nrm((DEPTH, D_PLE, d), D_PLE ** -0.5)
    ab_w_in = nrm((N_EVEN, d, AB_PROJ), d ** -0.5)
    a_sinks = nrm((N_EVEN, A_HEADS), 0.5)
    b_conv_w = nrm((N_EVEN, B_CONV, B_WIDTH), B_CONV ** -0.5)
    b_conv_b = nrm((N_EVEN, B_WIDTH), 0.02)
    b_wa = nrm((N_EVEN, B_BLOCKS, B_BLOCK, B_BLOCK), B_BLOCK ** -0.5)
    b_ba = nrm((N_EVEN, B_WIDTH), 0.02)
    b_wx = nrm((N_EVEN, B_BLOCKS, B_BLOCK, B_BLOCK), B_BLOCK ** -0.5)
    b_bx = nrm((N_EVEN, B_WIDTH), 0.02)
    a_c = jax.random.uniform(next(ks), (N_EVEN, B_WIDTH), f32, 0.9, 0.999)
    a0 = a_c ** (1.0 / RG_C)
    b_lam = jnp.log(a0) - jnp.log1p(-a0)
    ab_w_out = nrm((N_EVEN, A_WIDTH + B_WIDTH, d), DN_BETA * (A_WIDTH + B_WIDTH) ** -0.5)
    c_w_in = nrm((N_ODD, d, C_PROJ), d ** -0.5)
    c_conv_w = nrm((N_ODD, C_CONV, 3 * C_WIDTH), C_CONV ** -0.5)
    c_a_log = jnp.log(jax.random.uniform(next(ks), (N_ODD, C_HEADS), f32, 1.0, 16.0))
    dt = jnp.exp(jax.random.uniform(next(ks), (N_ODD, C_HEADS), f32, np.log(1e-3), np.log(1e-1)))
    c_dt_bias = dt + jnp.log(-jnp.expm1(-dt))
    c_norm_g = 1.0 + nrm((N_ODD, C_HEAD_DIM), 0.02)
    c_w_out = nrm((N_ODD, C_WIDTH, d), DN_BETA * C_WIDTH ** -0.5)
    return {'x': x, 'p': p,
            'ffn1_wg': ffn1_wg, 'ffn1_wu': ffn1_wu, 'ffn1_wd': ffn1_wd,
            'ffn2_wg': ffn2_wg, 'ffn2_wu': ffn2_wu, 'ffn2_wd': ffn2_wd,
            'ln_g': ln_g, 'ln_b': ln_b,
            'ple_wg': ple_wg, 'ple_bg': ple_bg, 'ple_wp': ple_wp,
            'ab_w_in': ab_w_in, 'a_sinks': a_sinks,
            'b_conv_w': b_conv_w, 'b_conv_b': b_conv_b,
            'b_wa': b_wa, 'b_ba': b_ba, 'b_wx': b_wx, 'b_bx': b_bx, 'b_lam': b_lam,
            'ab_w_out': ab_w_out,
            'c_w_in': c_w_in, 'c_conv_w': c_conv_w, 'c_a_log': c_a_log, 'c_dt_bias': c_dt_bias,
            'c_norm_g': c_norm_g, 'c_w_out': c_w_out}


def reference(x, p, ffn1_wg, ffn1_wu, ffn1_wd, ffn2_wg, ffn2_wu, ffn2_wd, ln_g, ln_b,
              ple_wg, ple_bg, ple_wp, ab_w_in, a_sinks, b_conv_w, b_conv_b,
              b_wa, b_ba, b_wx, b_bx, b_lam, ab_w_out,
              c_w_in, c_conv_w, c_a_log, c_dt_bias, c_norm_g, c_w_out):
    for i in range(DEPTH):
        j = i // 2
        x = layer_norm(DN_ALPHA * x + 0.5 * swiglu(x, ffn1_wg[i], ffn1_wu[i], ffn1_wd[i]), ln_g[i, 0], ln_b[i, 0])
        if i % 2 == 0:
            y = mixer_ab(x, ab_w_in[j], a_sinks[j], b_conv_w[j], b_conv_b[j],
                         b_wa[j], b_ba[j], b_wx[j], b_bx[j], b_lam[j], ab_w_out[j])
        else:
            y = mixer_c(x, c_w_in[j], c_conv_w[j], c_a_log[j], c_dt_bias[j], c_norm_g[j], c_w_out[j])
        x = layer_norm(DN_ALPHA * x + y, ln_g[i, 1], ln_b[i, 1])
        x = layer_norm(DN_ALPHA * x + 0.5 * swiglu(x, ffn2_wg[i], ffn2_wu[i], ffn2_wd[i]), ln_g[i, 2], ln_b[i, 2])
        x = x + jax.nn.sigmoid(x @ ple_wg[i] + ple_bg[i]) * (p[i] @ ple_wp[i])
    return x
```

```python
import contextlib
import numpy as np
import concourse.bass as bass
import concourse.mybir as mybir
from concourse.bass_utils import run_bass_kernel_spmd

F32 = mybir.dt.float32
BF16 = mybir.dt.bfloat16
ALU = mybir.AluOpType
AF = mybir.ActivationFunctionType
AX = mybir.AxisListType

D = 1024
S = 2048
DFF = 2816
NFC = 22
DPLE = 256
DEPTH = 2
ALPHA = (2.0 * DEPTH) ** 0.25
LN_EPS = 1e-5
NORM_EPS = 1e-6
NCORES = 8
SEQ_PER_CORE = 4

EPOCH = 30000
NDMA = 8


class _Ins:
    __slots__ = ("fn", "waits", "flag", "dma", "cum")

    def __init__(self, fn, dma=None):
        self.fn = fn
        self.waits = []
        self.flag = False
        self.dma = dma
        self.cum = 0


def _box(ap):
    t = ap.tensor
    dims = ap.ap
    off = int(ap.offset)
    sz = mybir.dt.size(ap.dtype)
    sp = str(ap.space)
    if sp in ("SB", "PSUM"):
        rs, pc = dims[0]
        if rs == 0:
            p0 = 0
            f0 = off
        else:
            p0 = off // rs
            f0 = off - p0 * rs
        ext = 1
        for st, cn in dims[1:]:
            ext += (cn - 1) * abs(st)
        return (t.name, p0, p0 + pc, f0 * sz, (f0 + ext) * sz)
    ext = 1
    for st, cn in dims:
        ext += (cn - 1) * abs(st)
    return (t.name, 0, 1, off * sz, (off + ext) * sz)


class Prog:
    ENG = ["pe", "act", "dve", "pool", "sp"]

    def __init__(self, nc):
        self.nc = nc
        self.streams = {e: [] for e in self.ENG}
        self.track = {}
        self.waited_c = {e: {} for e in self.ENG}
        self.waited_d = {e: {} for e in self.ENG}
        self.dma_k = {e: 0 for e in self.ENG}
        self.out_dmas = []

    def _deps(self, eng, reads, writes):
        toks = set()
        for ap, is_w in [(a, False) for a in reads] + [(a, True) for a in writes]:
            name, p0, p1, f0, f1 = _box(ap)
            recs = self.track.get(name)
            if not recs:
                continue
            keep = []
            for r in recs:
                rp0, rp1, rf0, rf1, rtok, rw = r
                ov = not (rp1 <= p0 or p1 <= rp0 or rf1 <= f0 or f1 <= rf0)
                if ov and (rw or is_w):
                    toks.add(rtok)
                if is_w and ov and rp0 >= p0 and rp1 <= p1 and rf0 >= f0 and rf1 <= f1:
                    continue
                keep.append(r)
            self.track[name] = keep
        return toks

    def _register(self, tok, reads, writes):
        for ap, is_w in [(a, False) for a in reads] + [(a, True) for a in writes]:
            name, p0, p1, f0, f1 = _box(ap)
            recs = self.track.setdefault(name, [])
            if not is_w and tok[0] == "c":
                for i, r in enumerate(recs):
                    if (not r[5]) and r[0] == p0 and r[1] == p1 and r[2] == f0 and r[3] == f1 \
                            and r[4][0] == "c" and r[4][1] == tok[1]:
                        recs[i] = (p0, p1, f0, f1, tok, False)
                        break
                else:
                    recs.append((p0, p1, f0, f1, tok, False))
            else:
                recs.append((p0, p1, f0, f1, tok, is_w))

    def _add_waits(self, eng, ins, toks):
        for tok in toks:
            if tok[0] == "c":
                _, f, idx = tok
                if f == eng and eng == "pe":
                    continue
                if self.waited_c[eng].get(f, -1) >= idx:
                    continue
                self.waited_c[eng][f] = idx
                self.streams[f][idx].flag = True
                ins.waits.append(tok)
            else:
                _, q, k = tok
                key = (q, k % NDMA)
                if self.waited_d[eng].get(key, -1) >= k:
                    continue
                self.waited_d[eng][key] = k
                ins.waits.append(tok)

    def op(self, eng, fn, reads=(), writes=()):
        reads = [a for a in reads if a is not None and not isinstance(a, (int, float))]
        writes = [a for a in writes if a is not None]
        ins = _Ins(fn)
        toks = self._deps(eng, reads, writes)
        self._add_waits(eng, ins, toks)
        idx = len(self.streams[eng])
        self.streams[eng].append(ins)
        self._register(("c", eng, idx), reads, writes)
        return ins

    def dma(self, q, out, in_, is_output=False, **kw):
        k = self.dma_k[q]
        self.dma_k[q] += 1
        ins = _Ins(lambda e: e.dma_start(out=out, in_=in_, **kw), dma=(q, k))
        toks = self._deps(q, [in_], [out])
        if k >= NDMA:
            toks.add(("d", q, k - NDMA))
        self._add_waits(q, ins, toks)
        self.streams[q].append(ins)
        tok = ("d", q, k)
        self._register(tok, [in_], [out])
        if is_output:
            self.out_dmas.append(tok)
        return ins

    def mm(self, out, lhsT, rhs, start=True, stop=True):
        return self.op("pe", lambda e: e.matmul(out, lhsT, rhs, start=start, stop=stop),
                       [lhsT, rhs], [out])

    def transpose(self, out, in_, ident):
        return self.op("pe", lambda e: e.transpose(out, in_, ident), [in_, ident], [out])

    def act(self, out, in_, func, bias=0.0, scale=1.0, accum_out=None):
        kw = {}
        if accum_out is not None:
            kw["accum_out"] = accum_out
        return self.op("act", lambda e: e.activation(out, in_, func, bias=bias, scale=scale, **kw),
                       [in_, bias, scale], [out, accum_out])

    def tt(self, eng, out, in0, in1, op):
        return self.op(eng, lambda e: e.tensor_tensor(out, in0, in1, op), [in0, in1], [out])

    def ts(self, eng, out, in0, s1, s2=None, op0=ALU.mult, op1=None):
        kw = {}
        if op1 is not None:
            kw["op1"] = op1
        return self.op(eng, lambda e: e.tensor_scalar(out, in0, s1, s2, op0, **kw),
                       [in0, s1, s2], [out])

    def stt(self, eng, out, in0, scalar, in1, op0, op1):
        return self.op(eng, lambda e: e.scalar_tensor_tensor(out, in0, scalar, in1, op0, op1),
                       [in0, scalar, in1], [out])

    def copy(self, eng, out, in_):
        if eng == "act":
            return self.op("act", lambda e: e.copy(out, in_), [in_], [out])
        return self.op(eng, lambda e: e.tensor_copy(out, in_), [in_], [out])

    def memset(self, eng, out, val):
        return self.op(eng, lambda e: e.memset(out, val), [], [out])

    def reduce(self, eng, out, in_, op, axis=AX.X):
        return self.op(eng, lambda e: e.tensor_reduce(out, in_, axis, op), [in_], [out])

    def recip(self, out, in_):
        return self.op("dve", lambda e: e.reciprocal(out, in_), [in_], [out])

    def scan(self, out, d0, d1, init, op0, op1):
        return self.op("dve", lambda e: e.tensor_tensor_scan(out, d0, d1, init, op0, op1),
                       [d0, d1, init], [out])

    def emit(self):
        nc = self.nc
        fin = _Ins(lambda e: None)
        self._add_waits("sp", fin, set(self.out_dmas))
        self.streams["sp"].append(fin)
        nsem = {}
        for e in self.ENG:
            cum = 0
            for ins in self.streams[e]:
                if ins.flag:
                    cum += 1
                ins.cum = cum
            nsem[e] = (cum + EPOCH - 1) // EPOCH
        with contextlib.ExitStack() as es:
            csem = {e: [es.enter_context(nc.semaphore(f"c_{e}_{i}")) for i in range(nsem[e])]
                    for e in self.ENG}
            dsem = {e: [es.enter_context(nc.semaphore(f"d_{e}_{i}")) for i in range(NDMA)]
                    for e in self.ENG if self.dma_k[e] > 0}
            block = es.enter_context(nc.Block())
            streams = self.streams

            def run(e, eh):
                for ins in streams[e]:
                    for tok in ins.waits:
                        if tok[0] == "c":
                            c = streams[tok[1]][tok[2]].cum
                            eh.wait_ge(csem[tok[1]][(c - 1) // EPOCH], (c - 1) % EPOCH + 1)
                        else:
                            _, q, k = tok
                            eh.wait_ge(dsem[q][k % NDMA], 16 * (k // NDMA + 1))
                    r = ins.fn(eh)
                    if r is None:
                        continue
                    if ins.dma is not None:
                        q, k = ins.dma
                        r.then_inc(dsem[q][k % NDMA], 16)
                    elif ins.flag:
                        c = ins.cum
                        r.then_inc(csem[e][(c - 1) // EPOCH], 1)

            @block.tensor
            def _(eh):
                run("pe", eh)

            @block.scalar
            def _(eh):
                run("act", eh)

            @block.vector
            def _(eh):
                run("dve", eh)

            @block.gpsimd
            def _(eh):
                run("pool", eh)

            @block.sync
            def _(eh):
                run("sp", eh)


class _Blob:
    def __init__(self):
        self.parts = []
        self.off = {}
        self.n = 0

    def add(self, name, arr):
        a = np.ascontiguousarray(arr, dtype=np.float32).reshape(-1)
        assert a.size % 128 == 0, name
        self.off[name] = (self.n, a.size)
        self.parts.append(a)
        self.n += a.size

    def cat(self):
        return np.concatenate(self.parts)


def _wlayout(inp):
    B = _Blob()
    for l in range(DEPTH):
        for f in (1, 2):
            wg = inp[f"ffn{f}_wg"][l].reshape(8, 128, NFC, 128)
            wu = inp[f"ffn{f}_wu"][l].reshape(8, 128, NFC, 128)
            gu = np.stack([wg, wu], 0)
            B.add(f"gu{l}{f}", gu.transpose(3, 2, 0, 1, 4))
            wd = inp[f"ffn{f}_wd"][l].reshape(NFC, 128, 8, 128)
            B.add(f"wd{l}{f}", wd.transpose(2, 1, 0, 3))
        pg = inp["ple_wg"][l].reshape(8, 128, 8, 128)
        pp = inp["ple_wp"][l].reshape(2, 128, 8, 128)
        B.add(f"ple{l}", np.concatenate([pg, pp], 0).transpose(2, 1, 0, 3))
    wi = inp["ab_w_in"][0].reshape(8, 128, 1792)
    B.add("abq", wi[:, :, 0:512].reshape(8, 128, 8, 64).transpose(2, 1, 0, 3))
    B.add("abk", wi[:, :, 512:640].reshape(8, 128, 2, 64).transpose(2, 1, 0, 3))
    B.add("abv", wi[:, :, 640:768].transpose(1, 0, 2))
    B.add("abx", wi[:, :, 768:1280].reshape(8, 128, 4, 128).transpose(2, 1, 0, 3))
    B.add("abg", wi[:, :, 1280:1792].reshape(8, 128, 4, 128).transpose(2, 1, 0, 3))
    wo = inp["ab_w_out"][0]
    B.add("aboa", wo[:512].reshape(8, 64, 8, 128).transpose(2, 1, 0, 3))
    B.add("abob", wo[512:].reshape(4, 128, 8, 128).transpose(2, 1, 0, 3))
    for nm, key in (("abwa", "b_wa"), ("abwx", "b_wx")):
        w = inp[key][0]
        bd = np.zeros((4, 128, 128), np.float32)
        for c in range(4):
            for hb in range(2):
                bd[c, hb * 64:(hb + 1) * 64, hb * 64:(hb + 1) * 64] = w[2 * c + hb]
        B.add(nm, bd)
    ci = inp["c_w_in"][0].reshape(8, 128, 4112)
    B.add("cw", ci[:, :, :4096].reshape(8, 128, 4, 8, 128).transpose(3, 2, 1, 0, 4))
    B.add("cba", ci[:, :, 4096:4112].transpose(1, 0, 2))
    B.add("cwo", inp["c_w_out"][0].reshape(8, 128, 8, 128))
    return B


def _clayout(inp):
    cols = {}
    parts = []
    n = 0

    def add(name, a):
        nonlocal n
        a = np.ascontiguousarray(a, dtype=np.float32).reshape(128, -1)
        cols[name] = (n, a.shape[1])
        parts.append(a)
        n += a.shape[1]

    add("ln_g", inp["ln_g"].reshape(6, 8, 128).transpose(2, 0, 1))
    add("ln_b", inp["ln_b"].reshape(6, 8, 128).transpose(2, 0, 1))
    add("ple_bg", inp["ple_bg"].reshape(2, 8, 128).transpose(2, 0, 1))
    add("ones", np.ones((128, 128), np.float32))
    add("ident", np.eye(128, dtype=np.float32))
    pidx = np.arange(128)
    sk = inp["a_sinks"][0]
    add("sink", np.stack([sk[2 * hp + pidx // 64] for hp in range(4)], 1))
    slopes = (2.0 ** (-8.0 * np.arange(1, 9, dtype=np.float32) / 8)).astype(np.float32)
    dist = np.abs((pidx % 64)[:, None] + 128 - np.arange(192)[None, :]).astype(np.float32)
    add("abias", np.stack([-(slopes[2 * hp + pidx // 64][:, None] * dist) for hp in range(4)], 1))
    add("convw", inp["b_conv_w"][0].reshape(4, 4, 128).transpose(2, 1, 0))
    add("convb", inp["b_conv_b"][0].reshape(4, 128).T)
    add("bba", inp["b_ba"][0].reshape(4, 128).T)
    add("bbx", inp["b_bx"][0].reshape(4, 128).T)
    add("blam", inp["b_lam"][0].reshape(4, 128).T)
    add("cconv", inp["c_conv_w"][0].reshape(4, 24, 128).transpose(2, 1, 0))
    add("alog", np.broadcast_to(inp["c_a_log"][0][None, :], (128, 8)))
    add("dtb", np.broadcast_to(inp["c_dt_bias"][0][None, :], (128, 8)))
    add("normg", inp["c_norm_g"][0].reshape(128, 1))
    kk = np.arange(128)[:, None]
    ii = np.arange(64)[None, :]
    add("maskU", (kk <= ii).astype(np.float32))
    add("maskneg", np.where(ii >= kk, 0.0, -1e30).astype(np.float32))
    add("nstrict", np.where(ii > kk, -1.0, 0.0).astype(np.float32))
    k2 = np.arange(128)[:, None]
    i2 = np.arange(128)[None, :]
    same = (k2 // 64) == (i2 // 64)
    add("maskUbd", (same & ((k2 % 64) <= (i2 % 64))).astype(np.float32))
    add("masknegbd", np.where(same & ((i2 % 64) >= (k2 % 64)), 0.0, -1e30).astype(np.float32))
    add("nstrictbd", np.where(same & ((i2 % 64) > (k2 % 64)), -1.0, 0.0).astype(np.float32))
    return np.concatenate(parts, 1), cols


def build_program(nseq, woff, wtotal, ccols, nccols, stop_after=None):
    nc = bass.Bass("TRN2", target_bir_lowering=False)
    P = Prog(nc)
    xT = nc.dram_tensor("xT", [nseq, D, S], F32, kind="ExternalInput").ap()
    pT = nc.dram_tensor("pT", [DEPTH, nseq, DPLE, S], F32, kind="ExternalInput").ap()
    wblob = nc.dram_tensor("wblob", [wtotal], F32, kind="ExternalInput").ap()
    cblob = nc.dram_tensor("cblob", [128, nccols], F32, kind="ExternalInput").ap()
    outT = nc.dram_tensor("outT", [nseq, D, S], F32, kind="ExternalOutput").ap()
    wsc = nc.dram_tensor("wsc", [wtotal], BF16, kind="Internal").ap()

    def wview(name, pattern, **kw):
        o, n = woff[name]
        return wsc[o:o + n].rearrange(pattern, **kw)

    with contextlib.ExitStack() as es:
        def sb(name, shape, dt):
            return es.enter_context(nc.sbuf_tensor(name, shape, dt))

        X = sb("X", [128, 8, S], F32)
        XB = sb("XB", [128, 8, S], BF16)
        CST = sb("CST", [128, nccols], F32)
        ONESB = sb("ONESB", [128, 128], BF16)
        WGU = [sb(f"WGU{i}", [128, 2, 8, 128], BF16) for i in range(3)]
        WD = [sb(f"WD{i}", [128, NFC, 128], BF16) for i in range(2)]
        AR_BYTES = 80 * 1024
        ARENA = sb("ARENA", [128, AR_BYTES // 2], BF16)
        PS = es.enter_context(nc.psum_tensor("PS", [128, 8, 512], F32))

        def carve(off_bytes, shape, dt, parts=128):
            n = int(np.prod(shape[1:]))
            szb = 2 if dt == BF16 else 4
            assert off_bytes % 4 == 0 and off_bytes + n * szb <= AR_BYTES, (off_bytes, shape)
            a = ARENA[0:parts, off_bytes // 2: off_bytes // 2 + n * szb // 2]
            if dt == F32:
                a = a.bitcast(F32)
            if len(shape) == 3:
                a = a.rearrange("p (a b) -> p a b", a=shape[1])
            elif len(shape) == 4:
                a = a.rearrange("p (a b c) -> p a b c", a=shape[1], b=shape[2])
            return a

        def cst(name, j=0, n=1):
            o, w = ccols[name]
            return CST[:, o + j:o + j + n]

        P.dma("sp", CST[:], cblob)
        o, w = ccols["ones"]
        P.copy("dve", ONESB[:], CST[:, o:o + 128])
        CH = 1 << 21
        pos = 0
        while pos < wtotal:
            n = min(CH, wtotal - pos)
            P.dma("pool", wsc[pos:pos + n].rearrange("(p f) -> p f", p=128),
                  wblob[pos:pos + n].rearrange("(p f) -> p f", p=128))
            pos += n

        wslot = [0]

        def pump(gen, n=1):
            if gen is None:
                return None
            for _ in range(n):
                try:
                    next(gen)
                except StopIteration:
                    return None
            return gen

        def drain(gen):
            while gen is not None:
                gen = pump(gen, 64)

        def layer_norm_gen(t0, nt, l, i, eps):
            for tt in range(nt):
                ts_ = slice(t0 + tt * 512, t0 + (tt + 1) * 512)
                RB = carve(64 * 1024, [128, 8, 512], BF16)
                RQ = carve(72 * 1024, [128, 8, 512], BF16)
                xs = X[:, :, ts_]
                for hh in range(2):
                    P.act(RB[:, hh * 4:(hh + 1) * 4, :], X[:, hh * 4:(hh + 1) * 4, ts_], AF.Copy)
                    yield
                    P.act(RQ[:, hh * 4:(hh + 1) * 4, :], X[:, hh * 4:(hh + 1) * 4, ts_], AF.Square)
                    yield
                for dc in range(8):
                    P.mm(PS[:, 6, :], ONESB[:], RB[:, dc, :], start=(dc == 0), stop=(dc == 7))
                for dc in range(8):
                    P.mm(PS[:, 7, :], ONESB[:], RQ[:, dc, :], start=(dc == 0), stop=(dc == 7))
                yield
                ST = carve(44 * 1024 + 2048 + (tt % 2) * 6144, [128, 3, 512], F32)
                M, V, R = ST[:, 0, :], ST[:, 1, :], ST[:, 2, :]
                P.ts("dve", M, PS[:, 6, :], 1.0 / D, None, op0=ALU.mult)
                P.tt("dve", V, M, M, ALU.mult)
                P.stt("dve", V, PS[:, 7, :], 1.0 / D, V, ALU.mult, ALU.subtract)
                P.ts("dve", V, V, eps, None, op0=ALU.add)
                P.act(R, V, AF.Sqrt)
                P.recip(R, R)
                yield
                for hh in range(2):
                    xh = X[:, hh * 4:(hh + 1) * 4, ts_]
                    P.tt("dve", xh, xh, M.unsqueeze(1).to_broadcast([128, 4, 512]), ALU.subtract)
                    yield
                    P.tt("dve", xh, xh, R.unsqueeze(1).to_broadcast([128, 4, 512]), ALU.mult)
                    yield
                for dc in range(8):
                    xd = X[:, dc, ts_]
                    gcol = cst("ln_g", (l * 3 + i) * 8 + dc)
                    bcol = cst("ln_b", (l * 3 + i) * 8 + dc)
                    P.act(XB[:, dc, ts_], xd, AF.Identity, bias=bcol, scale=gcol)
                    P.act(xd, xd, AF.Identity, bias=bcol, scale=gcol)
                    if dc % 2:
                        yield

        def layer_norm(t0, nt, l, i, eps):
            drain(layer_norm_gen(t0, nt, l, i, eps))

        def ffn(l, f, lni, bg=None):
            gu = wview(f"gu{l}{f}", "(fc p r) -> fc p r", fc=NFC, p=128)
            wd = wview(f"wd{l}{f}", "(dc p r) -> dc p r", dc=8, p=128)
            H = carve(0, [128, NFC, 1024], BF16)
            SG = carve(44 * 1024, [128, 2, 512], BF16)
            for half in range(2):
                t0 = half * 1024
                for fc in range(NFC):
                    wb = WGU[wslot[0] % 3]
                    wslot[0] += 1
                    P.dma("sp", wb[:].rearrange("p a b c -> p (a b c)"), gu[fc])
                    for sub in range(2):
                        tsl = slice(t0 + sub * 512, t0 + (sub + 1) * 512)
                        bg_, bu = (0, 1) if (fc * 2 + sub) % 2 == 0 else (2, 3)
                        for kc in range(8):
                            P.mm(PS[:, bg_, :], wb[:, 0, kc, :], XB[:, kc, tsl], start=(kc == 0), stop=(kc == 7))
                        for kc in range(8):
                            P.mm(PS[:, bu, :], wb[:, 1, kc, :], XB[:, kc, tsl], start=(kc == 0), stop=(kc == 7))
                        sg = SG[:, (fc * 2 + sub) % 2, :]
                        P.act(sg, PS[:, bg_, :], AF.Silu)
                        P.tt("dve", H[:, fc, sub * 512:(sub + 1) * 512], sg, PS[:, bu, :], ALU.mult)
                        bg = pump(bg, 2)
                drain(bg)
                for dc in range(8):
                    wdb = WD[dc % 2]
                    P.dma("sp", wdb[:].rearrange("p a b -> p (a b)"), wd[dc])
                    for sub in range(2):
                        tsl = slice(t0 + sub * 512, t0 + (sub + 1) * 512)
                        bank = 4 + (dc * 2 + sub) % 2
                        for fc in range(NFC):
                            P.mm(PS[:, bank, :], wdb[:, fc, :], H[:, fc, sub * 512:(sub + 1) * 512],
                                 start=(fc == 0), stop=(fc == NFC - 1))
                        xs = X[:, dc, tsl]
                        P.stt("dve", xs, PS[:, bank, :], 0.5 / ALPHA, xs, ALU.mult, ALU.add)
                bg = layer_norm_gen(t0, 2, l, lni, LN_EPS / (ALPHA * ALPHA))
            return bg

        def ple(l, s, bg=None):
            pw = wview(f"ple{l}", "(dc p r) -> p dc r", dc=8, p=128)
            PT = carve(0, [128, 2, S], BF16)
            for kc in range(2):
                P.dma("pool", PT[:, kc, :], pT[l, s, kc * 128:(kc + 1) * 128, :])
            SGM = carve(8 * 1024, [128, 2, 512], F32)
            WP = carve(12 * 1024, [128, 8, 10, 128], BF16)
            for dc in range(8):
                P.dma("sp", WP[:, dc].rearrange("p a b -> p (a b)"), pw[:, dc, :])
            for tt in range(4):
                tsl = slice(tt * 512, (tt + 1) * 512)
                if tt == 2:
                    drain(bg)
                    bg = None
                for dc in range(8):
                    wb = WP[:, dc]
                    i2 = (dc * 4 + tt) % 2
                    for kc in range(8):
                        P.mm(PS[:, i2, :], wb[:, kc, :], XB[:, kc, tsl], start=(kc == 0), stop=(kc == 7))
                    for kc in range(2):
                        P.mm(PS[:, 2 + i2, :], wb[:, 8 + kc, :], PT[:, kc, tsl], start=(kc == 0), stop=(kc == 1))
                    sg = SGM[:, i2, :]
                    P.act(sg, PS[:, i2, :], AF.Sigmoid, bias=cst("ple_bg", l * 8 + dc))
                    P.tt("dve", sg, sg, PS[:, 2 + i2, :], ALU.mult)
                    P.tt("dve", X[:, dc, tsl], X[:, dc, tsl], sg, ALU.add)
                    bg = pump(bg, 2)
                for hh in range(2):
                    P.copy("act" if hh else "dve", XB[:, hh * 4:(hh + 1) * 4, tsl], X[:, hh * 4:(hh + 1) * 4, tsl])

        IDB = sb("IDB", [128, 128], BF16)
        o_, w_ = ccols["ident"]
        P.copy("dve", IDB[:], CST[:, o_:o_ + 128])
        C0 = sb("C0", [128, 8], F32)
        P.act(C0[:, 0:4], cst("blam", 0, 4), AF.Exp, scale=-1.0)
        P.act(C0[:, 0:4], C0[:, 0:4], AF.Ln, bias=1.0)
        P.ts("dve", C0[:, 4:8], C0[:, 0:4], -16.0, None, op0=ALU.mult)
        P.ts("dve", C0[:, 0:4], C0[:, 0:4], -8.0, None, op0=ALU.mult)

        def wload(view, shape_free):
            wb = WGU[wslot[0] % 3]
            wslot[0] += 1
            n = int(np.prod(shape_free))
            flat = wb[:].rearrange("p a b c -> p (a b c)")
            np_ = view.shape[0]
            P.dma("sp", flat[0:np_, 0:n], view)
            return flat[0:np_, 0:n]

        def resid_add(dc, tsl, bank, scale):
            xs = X[:, dc, tsl]
            P.stt("dve", xs, PS[:, bank, :], scale, xs, ALU.mult, ALU.add)

        def mixer_ab(s):
            bankc = [0]

            def nb(lo=0, hi=6):
                b = lo + bankc[0] % (hi - lo)
                bankc[0] += 1
                return b

            KT = carve(0, [64, 2, S], BF16, parts=64)
            V = carve(8 * 1024, [64, 32, 128], BF16, parts=64)
            QT = carve(16 * 1024, [64, 2, 2 * S], BF16, parts=64)
            YAT = carve(32 * 1024, [64, 8, S], BF16, parts=64)
            WK0 = 64 * 1024
            wk = wview("abk", "(kv p r) -> kv p r", kv=2, p=128)
            for kv in range(2):
                w = wload(wk[kv], [8, 64]).rearrange("p (a b) -> p a b", a=8)
                for tt in range(4):
                    tsl = slice(tt * 512, (tt + 1) * 512)
                    b = nb()
                    for kc in range(8):
                        P.mm(PS[0:64, b, :], w[:, kc, :], XB[:, kc, tsl], start=(kc == 0), stop=(kc == 7))
                    P.copy("act", KT[:, kv, tsl], PS[0:64, b, :])
            wv = wload(wview("abv", "(p r) -> p r", p=128), [8, 128]).rearrange("p (a b) -> p a b", a=8)
            for g in range(8):
                b = nb()
                for cc in range(4):
                    n = g * 4 + cc
                    for kc in range(8):
                        P.mm(PS[0:64, b, cc * 128:(cc + 1) * 128], XB[:, kc, n * 64:(n + 1) * 64], wv[:, kc, :],
                             start=(kc == 0), stop=(kc == 7))
                P.copy("act", V[:, g * 4:(g + 1) * 4, :], PS[0:64, b, :].rearrange("p (a b) -> p a b", a=4))
            wq = wview("abq", "(h p r) -> h p r", h=8, p=128)
            PSB = PS[:, 7, :].bitcast(BF16)
            for hp in range(4):
                kv = hp // 2
                for h2 in range(2):
                    w = wload(wq[2 * hp + h2], [8, 64]).rearrange("p (a b) -> p a b", a=8)
                    for tt in range(4):
                        tsl = slice(tt * 512, (tt + 1) * 512)
                        b = nb()
                        for kc in range(8):
                            P.mm(PS[0:64, b, :], w[:, kc, :], XB[:, kc, tsl], start=(kc == 0), stop=(kc == 7))
                        qv = QT[:, hp % 2, tt * 1024:(tt + 1) * 1024].rearrange("p (n h c) -> p n h c", n=8, h=2)
                        P.act(qv[:, :, h2, :], PS[0:64, b, :].rearrange("p (n c) -> p n c", n=8), AF.Copy, scale=0.125)
                for n in range(32):
                    j0 = max(0, 2 - n)
                    nj = 3 - j0
                    nk = nj * 64
                    k0 = (n - 2 + j0) * 64
                    i2 = n % 2
                    SBf = carve(WK0 + i2 * 768, [128, 192], F32)
                    PEf = carve(WK0 + 1536 + i2 * 768, [128, 192], F32)
                    PN = carve(WK0 + 3072 + i2 * 384, [128, 192], BF16)
                    PTs = carve(WK0 + 3840 + i2 * 768, [64, 3, 128], BF16, parts=64)
                    STt = carve(WK0 + 5376 + i2 * 32, [128, 8], F32)
                    b = nb()
                    P.mm(PS[:, b, 0:nk], QT[:, hp % 2, n * 128:(n + 1) * 128], KT[:, kv, k0:k0 + nk])
                    o_, w_ = ccols["abias"]
                    bias = CST[:, o_ + hp * 192 + j0 * 64: o_ + (hp + 1) * 192]
                    P.tt("dve", SBf[:, 0:nk], PS[:, b, 0:nk], bias, ALU.add)
                    P.reduce("dve", STt[:, 0:1], SBf[:, 0:nk], ALU.max)
                    P.ts("dve", STt[:, 1:2], STt[:, 0:1], cst("sink", hp), -1.0, op0=ALU.max, op1=ALU.mult)
                    P.act(PEf[:, 0:nk], SBf[:, 0:nk], AF.Exp, bias=STt[:, 1:2], accum_out=STt[:, 2:3])
                    P.act(STt[:, 3:4], cst("sink", hp), AF.Exp, bias=STt[:, 1:2])
                    P.tt("dve", STt[:, 4:5], STt[:, 2:3], STt[:, 3:4], ALU.add)
                    P.recip(STt[:, 5:6], STt[:, 4:5])
                    P.act(PN[:, 0:nk], PEf[:, 0:nk], AF.Copy, scale=STt[:, 5:6])
                    for jj in range(nj):
                        P.transpose(PSB[0:64, (i2 * 3 + jj) * 128:(i2 * 3 + jj + 1) * 128], PN[:, jj * 64:(jj + 1) * 64], IDB[:])
                    P.copy("act", PTs[:, 0:nj, :],
                           PSB[0:64, i2 * 384:i2 * 384 + nj * 128].rearrange("p (a b) -> p a b", a=nj))
                    ob = 6
                    oc = (n % 4) * 128
                    for jj in range(nj):
                        P.mm(PS[0:64, ob, oc:oc + 128], V[:, n - 2 + j0 + jj, kv * 64:(kv + 1) * 64], PTs[:, jj, :],
                             start=(jj == 0), stop=(jj == nj - 1))
                    if n % 4 == 3:
                        n0 = n - 3
                        P.copy("act", YAT[:, 2 * hp:2 * hp + 2, n0 * 64:(n0 + 4) * 64].rearrange("p h (n c) -> p h n c", n=4),
                               PS[0:64, ob, :].rearrange("p (n h c) -> p h n c", n=4, h=2))
            woa = wview("aboa", "(dc p r) -> dc p r", dc=8, p=64)
            for dc in range(8):
                w = wload(woa[dc], [8, 128]).rearrange("p (a b) -> p a b", a=8)
                for tt in range(4):
                    tsl = slice(tt * 512, (tt + 1) * 512)
                    b = nb()
                    for h in range(8):
                        P.mm(PS[:, b, :], w[:, h, :], YAT[:, h, tsl], start=(h == 0), stop=(h == 7))
                    resid_add(dc, tsl, b, 1.0 / ALPHA)
            BXP = carve(0, [128, 2052], F32)
            BXC = carve(8208, [128, S], F32)
            RG = carve(16400, [128, S], F32)
            IG = carve(24592, [128, S], F32)
            GG = carve(32784, [128, S], F32)
            AA = carve(40976, [128, S], F32)
            BXB = carve(49168, [128, S], BF16)
            YBT = carve(53264, [128, 4, S], BF16)
            wx = wview("abx", "(c p r) -> c p r", c=4, p=128)
            wg_ = wview("abg", "(c p r) -> c p r", c=4, p=128)
            wa_ = wview("abwa", "(c p r) -> c p r", c=4, p=128)
            wxx = wview("abwx", "(c p r) -> c p r", c=4, p=128)
            P.memset("pool", BXP[:, 0:3], 0.0)
            for c in range(4):
                w = wload(wx[c], [8, 128]).rearrange("p (a b) -> p a b", a=8)
                for tt in range(4):
                    tsl = slice(tt * 512, (tt + 1) * 512)
                    b = nb()
                    for kc in range(8):
                        P.mm(PS[:, b, :], w[:, kc, :], XB[:, kc, tsl], start=(kc == 0), stop=(kc == 7))
                    P.copy("act", BXP[:, 3 + tt * 512:3 + (tt + 1) * 512], PS[:, b, :])
                w = wload(wg_[c], [8, 128]).rearrange("p (a b) -> p a b", a=8)
                for tt in range(4):
                    tsl = slice(tt * 512, (tt + 1) * 512)
                    b = nb()
                    for kc in range(8):
                        P.mm(PS[:, b, :], w[:, kc, :], XB[:, kc, tsl], start=(kc == 0), stop=(kc == 7))
                    P.act(GG[:, tsl], PS[:, b, :], AF.Gelu)
                o_, w_ = ccols["convw"]
                cw = lambda j: CST[:, o_ + c * 4 + j:o_ + c * 4 + j + 1]
                P.ts("dve", BXC[:], BXP[:, 0:S], cw(0), cst("convb", c), op0=ALU.mult, op1=ALU.add)
                P.stt("dve", BXC[:], BXP[:, 1:1 + S], cw(1), BXC[:], ALU.mult, ALU.add)
                P.stt("dve", BXC[:], BXP[:, 2:2 + S], cw(2), BXC[:], ALU.mult, ALU.add)
                P.stt("dve", BXC[:], BXP[:, 3:3 + S], cw(3), BXC[:], ALU.mult, ALU.add)
                P.copy("act", BXB[:], BXC[:])
                wa = wload(wa_[c], [128])
                wx2 = wload(wxx[c], [128])
                for tt in range(4):
                    tsl = slice(tt * 512, (tt + 1) * 512)
                    b = nb()
                    P.mm(PS[:, b, :], wa, BXB[:, tsl])
                    P.act(RG[:, tsl], PS[:, b, :], AF.Sigmoid, bias=cst("bba", c))
                    b = nb()
                    P.mm(PS[:, b, :], wx2, BXB[:, tsl])
                    P.act(IG[:, tsl], PS[:, b, :], AF.Sigmoid, bias=cst("bbx", c))
                P.act(AA[:], RG[:], AF.Exp, scale=C0[:, c:c + 1])
                P.act(RG[:], RG[:], AF.Exp, scale=C0[:, 4 + c:5 + c])
                P.act(RG[:], RG[:], AF.Sqrt, bias=1.0, scale=-1.0)
                P.tt("dve", IG[:], IG[:], BXC[:], ALU.mult)
                P.tt("dve", IG[:], IG[:], RG[:], ALU.mult)
                P.scan(RG[:], AA[:], IG[:], 0.0, ALU.mult, ALU.add)
                P.tt("dve", YBT[:, c, :], RG[:], GG[:], ALU.mult)
            wob = wview("abob", "(dc p r) -> dc p r", dc=8, p=128)
            for dc in range(8):
                w = wload(wob[dc], [4, 128]).rearrange("p (a b) -> p a b", a=4)
                for tt in range(4):
                    tsl = slice(tt * 512, (tt + 1) * 512)
                    b = nb()
                    for c in range(4):
                        P.mm(PS[:, b, :], w[:, c, :], YBT[:, c, tsl], start=(c == 0), stop=(c == 3))
                    resid_add(dc, tsl, b, 1.0 / ALPHA)
            drain(layer_norm_gen(0, 2, 0, 1, LN_EPS / (ALPHA * ALPHA)))
            return layer_norm_gen(1024, 2, 0, 1, LN_EPS / (ALPHA * ALPHA))

        NEGA = sb("NEGA", [128, 8], F32)
        P.act(NEGA[:], cst("alog", 0, 8), AF.Exp)
        P.ts("dve", NEGA[:], NEGA[:], -1.0, None, op0=ALU.mult)

        def mixer_c(s):
            bankc = [0]

            def nb(lo=0, hi=6):
                b = lo + bankc[0] % (hi - lo)
                bankc[0] += 1
                return b

            def bc_last(ap, n):
                return ap.unsqueeze(len(ap.shape)).to_broadcast(list(ap.shape) + [n])

            def bc_mid(ap, n):
                return ap.unsqueeze(1).to_broadcast([ap.shape[0], n, ap.shape[1]])

            GBR = carve(0, [128, 16, 16], F32)
            BETA = carve(1024, [128, 16, 8], F32)
            GNEG = carve(1536, [128, 16, 8], F32)
            CIN = carve(4096, [128, 2052], F32)
            DECT = carve(4112, [128, 16, 128], F32)
            QN = carve(12304, [128, S], F32)
            KN = carve(20496, [128, S], F32)
            NKGT = carve(28688, [128, S], BF16)
            QG = carve(28688 + 4096, [128, S], BF16)
            GCB = carve(36880, [128, 32, 64], F32)
            OT = carve(36880, [128, S], F32)
            KDEC = carve(45072, [128, 16, 128], BF16)
            VTOK = carve(49168, [128, 16, 128], BF16)
            ATT = carve(53264, [128, 16, 128], BF16)
            ZB = carve(57360, [128, 16, 128], BF16)
            GU = carve(61456, [128, 16, 128], F32)
            XI = carve(69648, [128, 4, 128], F32)
            XTI = carve(71696, [128, 4, 128], F32)
            PM = carve(73744, [128, 4, 128], F32)
            TMPA = carve(69648, [128, 512], F32)
            TMPB = carve(71696, [128, 512], F32)
            SM0 = 75792
            GL = carve(SM0, [128, 32], F32)
            GCC = carve(SM0 + 128, [128, 16], F32)
            GH = carve(SM0 + 192, [128, 16], F32)
            BH = carve(SM0 + 256, [128, 16], F32)
            S2 = carve(SM0 + 320, [128, 16], F32)
            SF = carve(SM0 + 384, [128, 128], F32)
            SBb = carve(SM0 + 896, [128, 128], BF16)
            RB = carve(SM0 + 1152, [128, 128], BF16)
            VN = carve(SM0 + 1408, [128, 128], BF16)
            SQB = carve(28688 + 4096, [128, S], BF16)
            OGB = carve(28688, [128, S], BF16)
            VSB = WD[0][:].rearrange("p a b -> p (a b)")[:, 0:S]

            o_, w_ = ccols["ones"]
            ONESF = CST[:, o_:o_ + 128]
            o_, w_ = ccols["ident"]
            IDF = CST[:, o_:o_ + 128]
            o_, w_ = ccols["maskUbd"]
            MU = CST[:, o_:o_ + 128]
            o_, w_ = ccols["masknegbd"]
            MNEG = CST[:, o_:o_ + 128]
            o_, w_ = ccols["nstrictbd"]
            NSTR = CST[:, o_:o_ + 128]

            wba = wload(wview("cba", "(p r) -> p r", p=128), [8, 16]).rearrange("p (a b) -> p a b", a=8)
            b = nb()
            for pr in range(16):
                for kc in range(8):
                    P.mm(PS[:, b, pr * 16:(pr + 1) * 16], XB[:, kc, pr * 128:(pr + 1) * 128], wba[:, kc, :],
                         start=(kc == 0), stop=(kc == 7))
            P.copy("act", GBR[:], PS[:, b, 0:256].rearrange("p (a b) -> p a b", a=16))
            P.act(BETA[:], GBR[:, :, 0:8], AF.Sigmoid)
            o_, w_ = ccols["dtb"]
            P.tt("dve", GNEG[:], GBR[:, :, 8:16], bc_mid(CST[:, o_:o_ + 8], 16), ALU.add)
            P.act(GNEG[:], GNEG[:], AF.Exp)
            P.act(GNEG[:], GNEG[:], AF.Ln, bias=1.0)
            P.tt("dve", GNEG[:], GNEG[:], bc_mid(NEGA[:, :], 16), ALU.mult)

            cwv = wview("cw", "(h g p r) -> h g p r", h=8, g=4, p=128)
            cwo = wview("cwo", "(h p r) -> h p r", h=8, p=128)
            oc_, w_ = ccols["cconv"]

            CINB = carve(4096, [128, 2052], BF16)
            DG = carve(4096 + 4104, [128, 4, 128], BF16)
            o_, w_ = ccols["ident"]
            IDFc = CST[:, o_:o_ + 128]

            def phaseA(h):
                P.memset("pool", CINB[:, 0:3], 0.0)
                for g, dest in ((0, QN), (1, KN), (2, None)):
                    w = wload(cwv[h, g], [8, 128]).rearrange("p (a b) -> p a b", a=8)
                    cw = lambda j: CST[:, oc_ + (g * 8 + h) * 4 + j: oc_ + (g * 8 + h) * 4 + j + 1]
                    for j in range(4):
                        P.ts("pool", DG[:, j, :], IDFc, cw(j), None, op0=ALU.mult)
                    for tt in range(4):
                        tsl = slice(tt * 512, (tt + 1) * 512)
                        b = nb()
                        for kc in range(8):
                            P.mm(PS[:, b, :], w[:, kc, :], XB[:, kc, tsl], start=(kc == 0), stop=(kc == 7))
                        P.copy("act", CINB[:, 3 + tt * 512:3 + (tt + 1) * 512], PS[:, b, :])
                        yield
                    SQ = carve(69648, [128, S], BF16)
                    for tt in range(4):
                        tsl = slice(tt * 512, (tt + 1) * 512)
                        b = nb()
                        for j in range(4):
                            P.mm(PS[:, b, :], DG[:, j, :], CINB[:, j + tt * 512:j + tt * 512 + 512],
                                 start=(j == 0), stop=(j == 3))
                        if g < 2:
                            P.act(dest[:, tsl], PS[:, b, :], AF.Silu)
                            P.act(SQ[:, tsl], dest[:, tsl], AF.Square)
                        else:
                            P.act(VSB[:, tsl], PS[:, b, :], AF.Silu)
                        yield
                    if g < 2:
                        for tt in range(4):
                            tsl = slice(tt * 512, (tt + 1) * 512)
                            b = nb()
                            P.mm(PS[:, b, :], ONESB[:], SQ[:, tsl])
                            rn = carve(73744, [128, 512], F32)
                            sc = 128.0 if g == 0 else 1.0
                            P.act(rn, PS[:, b, :], AF.Ln, bias=NORM_EPS * sc, scale=sc)
                            P.act(rn, rn, AF.Exp, scale=-0.5)
                            P.tt("pool", dest[:, tsl], dest[:, tsl], rn, ALU.mult)
                            yield

            gen = phaseA(0)
            drain(gen)
            for h in range(8):
                P.copy("pool", GH[:], GNEG[:, :, h])
                P.copy("pool", BH[:], BETA[:, :, h])
                P.tt("dve", GU[:], bc_mid(MU, 16), bc_last(GH[:], 128), ALU.mult)
                for q4 in range(4):
                    b = nb()
                    P.mm(PS[:, b, :], ONESF, GU[:, q4 * 4:(q4 + 1) * 4, :].rearrange("p a b -> p (a b)"))
                    P.copy("act", GCB[:, q4 * 8:(q4 + 1) * 8, :].rearrange("p a b -> p (a b)"), PS[:, b, :])
                b = nb()
                P.mm(PS[:, b, 0:16], MU, GH[:])
                P.copy("act", GCC[:], PS[:, b, 0:16])
                P.act(GL[:], GCB[:, :, 63], AF.Exp)
                GCBv = GCB[:].rearrange("p (a b) c -> p a b c", b=2)
                for par in range(2):
                    rows = slice(par * 64, (par + 1) * 64)
                    P.tt("dve", S2[rows, :], GCBv[rows, :, par, 63], GCC[rows, :], ALU.subtract)
                P.act(S2[:], S2[:], AF.Exp)
                for g4 in range(4):
                    b = nb()
                    for cc in range(4):
                        pr = g4 * 4 + cc
                        P.transpose(PS[:, b, cc * 128:(cc + 1) * 128], KN[:, pr * 128:(pr + 1) * 128], IDF)
                    P.tt("dve", KDEC[:, g4 * 4:(g4 + 1) * 4, :], PS[:, b, :].rearrange("p (a b) -> p a b", a=4),
                         bc_last(S2[:, g4 * 4:(g4 + 1) * 4], 128), ALU.mult)
                    b = nb()
                    PSv = PS[:, b, :].bitcast(BF16)
                    for cc in range(4):
                        pr = g4 * 4 + cc
                        P.transpose(PSv[:, cc * 128:(cc + 1) * 128], VSB[:, pr * 128:(pr + 1) * 128], IDB[:])
                    P.copy("act", VTOK[:, g4 * 4:(g4 + 1) * 4, :], PSv[:, 0:512].rearrange("p (a b) -> p a b", a=4))
                GCBp = GCB[:].rearrange("p (a b) c -> p a (b c)", b=2)
                P.tt("dve", DECT[:], GCBp, bc_last(GCC[:], 128), ALU.subtract)
                P.tt("dve", DECT[:], DECT[:], bc_mid(MNEG, 16), ALU.add)
                P.act(DECT[:], DECT[:], AF.Exp)
                GCBf = GCB[:].rearrange("p a b -> p (a b)")
                P.act(GCBf, GCBf, AF.Exp)
                P.stt("dve", NKGT[:], KN[:], -1.0, GCBf, ALU.mult, ALU.mult)
                P.tt("dve", QG[:], QN[:], GCBf, ALU.mult)
                for q4 in range(4):
                    b = nb()
                    for cc in range(4):
                        pr = q4 * 4 + cc
                        P.mm(PS[:, b, cc * 128:(cc + 1) * 128], KN[:, pr * 128:(pr + 1) * 128], QN[:, pr * 128:(pr + 1) * 128])
                    P.tt("dve", ATT[:, q4 * 4:(q4 + 1) * 4, :], PS[:, b, :].rearrange("p (a b) -> p a b", a=4),
                         DECT[:, q4 * 4:(q4 + 1) * 4, :], ALU.mult)
                P.tt("dve", DECT[:], DECT[:], bc_mid(NSTR, 16), ALU.mult)
                P.tt("pool", DECT[:], DECT[:], bc_last(BH[:], 128), ALU.mult)
                for q4 in range(4):
                    b = nb()
                    for cc in range(4):
                        pr = q4 * 4 + cc
                        P.mm(PS[:, b, cc * 128:(cc + 1) * 128], KN[:, pr * 128:(pr + 1) * 128], KN[:, pr * 128:(pr + 1) * 128])
                    P.tt("dve", XI[:], PS[:, b, :].rearrange("p (a b) -> p a b", a=4),
                         DECT[:, q4 * 4:(q4 + 1) * 4, :], ALU.mult)
                    b = nb()
                    for cc in range(4):
                        P.transpose(PS[:, b, cc * 128:(cc + 1) * 128], XI[:, cc, :], IDF)
                    P.copy("act", XTI[:], PS[:, b, :].rearrange("p (a b) -> p a b", a=4))
                    P.tt("dve", PM[:], XI[:], bc_mid(IDF, 4), ALU.add)
                    for it in range(5):
                        b1 = nb()
                        for cc in range(4):
                            P.mm(PS[:, b1, cc * 128:(cc + 1) * 128], XTI[:, cc, :], XI[:, cc, :])
                        b2 = nb()
                        for cc in range(4):
                            P.mm(PS[:, b2, cc * 128:(cc + 1) * 128], XI[:, cc, :], XTI[:, cc, :])
                        P.copy("act", XI[:], PS[:, b1, :].rearrange("p (a b) -> p a b", a=4))
                        P.copy("dve", XTI[:], PS[:, b2, :].rearrange("p (a b) -> p a b", a=4))
                        b3 = nb()
                        for cc in range(4):
                            P.mm(PS[:, b3, cc * 128:(cc + 1) * 128], XTI[:, cc, :], PM[:, cc, :])
                        P.tt("dve", PM[:], PM[:], PS[:, b3, :].rearrange("p (a b) -> p a b", a=4), ALU.add)
                    P.copy("act", ZB[:, q4 * 4:(q4 + 1) * 4, :], PM[:])
                P.memset("pool", SF[:], 0.0)
                P.memset("pool", SBb[:], 0.0)
                gen = phaseA(h + 1) if h < 7 else None
                for c in range(32):
                    gen = pump(gen, 2)
                    cc = c % 8
                    pr = c // 2
                    pb = (c % 2) * 64
                    rows = slice(pb, pb + 64)
                    csl = slice(c * 64, (c + 1) * 64)
                    ob = 6 + (c // 8) % 2
                    b1 = nb()
                    P.mm(PS[rows, b1, 0:128], NKGT[:, csl], SBb[:])
                    P.tt("dve", RB[rows, :], PS[rows, b1, 0:128], VTOK[rows, pr, :], ALU.add)
                    b2 = nb()
                    P.mm(PS[rows, b2, 0:128], ZB[rows, pr, pb:pb + 64], RB[rows, :])
                    P.ts("dve", VN[rows, :], PS[rows, b2, 0:128], BH[rows, pr:pr + 1], None, op0=ALU.mult)
                    P.mm(PS[:, ob, cc * 64:(cc + 1) * 64], SBb[:], QG[:, csl], start=True, stop=False)
                    P.mm(PS[:, ob, cc * 64:(cc + 1) * 64], VN[rows, :], ATT[rows, pr, pb:pb + 64], start=False, stop=True)
                    b3 = nb()
                    P.mm(PS[:, b3, 0:128], KDEC[rows, pr, :], VN[rows, :])
                    P.stt("dve", SF[:], SF[:], GL[:, c:c + 1], PS[:, b3, 0:128], ALU.mult, ALU.add)
                    P.copy("dve", SBb[:], SF[:])
                    if cc == 7:
                        P.copy("act", OT[:, (c - 7) * 64:(c + 1) * 64], PS[:, ob, :])
                drain(gen)
                P.act(SQB[:], OT[:], AF.Square)
                wz = wload(cwv[h, 3], [8, 128]).rearrange("p (a b) -> p a b", a=8)
                wo = wload(cwo[h], [8, 128]).rearrange("p (a b) -> p a b", a=8)
                for tt in range(4):
                    tsl = slice(tt * 512, (tt + 1) * 512)
                    b = nb()
                    P.mm(PS[:, b, :], ONESB[:], SQB[:, tsl])
                    P.ts("dve", TMPA[:], PS[:, b, :], 1.0 / 128, NORM_EPS, op0=ALU.mult, op1=ALU.add)
                    P.act(TMPA[:], TMPA[:], AF.Sqrt)
                    P.recip(TMPA[:], TMPA[:])
                    b = nb()
                    for kc in range(8):
                        P.mm(PS[:, b, :], wz[:, kc, :], XB[:, kc, tsl], start=(kc == 0), stop=(kc == 7))
                    P.act(TMPB[:], PS[:, b, :], AF.Silu)
                    P.stt("dve", TMPA[:], OT[:, tsl], cst("normg"), TMPA[:], ALU.mult, ALU.mult)
                    P.tt("dve", OGB[:, tsl], TMPA[:], TMPB[:], ALU.mult)
                    for dc in range(8):
                        b = nb()
                        P.mm(PS[:, b, :], wo[:, dc, :], OGB[:, tsl])
                        resid_add(dc, tsl, b, 1.0 / ALPHA)
            drain(layer_norm_gen(0, 2, 1, 1, LN_EPS / (ALPHA * ALPHA)))
            return layer_norm_gen(1024, 2, 1, 1, LN_EPS / (ALPHA * ALPHA))

        def mixer(l, s):
            if l == 0:
                return mixer_ab(s)
            return mixer_c(s)

        done = False
        for s in range(nseq):
            for dc in range(8):
                P.dma("sp", X[:, dc, :], xT[s, dc * 128:(dc + 1) * 128, :])
            for dc in range(8):
                P.copy("dve" if dc % 2 else "act", XB[:, dc, :], X[:, dc, :])
            for l in range(DEPTH):
                bg = ffn(l, 1, 0)
                drain(bg)
                if stop_after == (l, "ffn1"):
                    break
                bg = mixer(l, s)
                if stop_after == (l, "mix"):
                    drain(bg)
                    break
                bg = ffn(l, 2, 2, bg)
                if stop_after == (l, "ffn2"):
                    drain(bg)
                    break
                ple(l, s, bg)
                if stop_after == (l, "ple"):
                    break
            for dc in range(8):
                P.dma("sp", outT[s, dc * 128:(dc + 1) * 128, :], X[:, dc, :], is_output=True)
        P.emit()
    return nc


def _prep(inputs):
    inp = {k: np.asarray(v) for k, v in inputs.items()}
    B = _wlayout(inp)
    cb, ccols = _clayout(inp)
    return inp, B, cb, ccols


def kernel(**inputs):
    inp, B, cb, ccols = _prep(inputs)
    wb = B.cat()
    nc = build_program(SEQ_PER_CORE, B.off, B.n, ccols, cb.shape[1])
    x = inp["x"]
    p = inp["p"]
    in_maps = []
    for c in range(NCORES):
        sl = slice(c * SEQ_PER_CORE, (c + 1) * SEQ_PER_CORE)
        in_maps.append({
            "xT": np.ascontiguousarray(x[sl].transpose(0, 2, 1)),
            "pT": np.ascontiguousarray(p[:, sl].transpose(0, 1, 3, 2)),
            "wblob": wb,
            "cblob": cb,
        })
    res = run_bass_kernel_spmd(nc, in_maps, core_ids=list(range(NCORES)))
    out = np.concatenate([r["outT"] for r in res.results], 0)
    return np.ascontiguousarray(out.transpose(0, 2, 1)).astype(np.float32)
```

```python
import contextlib
import numpy as np
import concourse.bass as bass
import concourse.mybir as mybir
from concourse.bass_utils import run_bass_kernel_spmd

F32 = mybir.dt.float32
BF16 = mybir.dt.bfloat16
ALU = mybir.AluOpType
AF = mybir.ActivationFunctionType
AX = mybir.AxisListType

D = 1024
S = 2048
DFF = 2816
NFC = 22
DPLE = 256
DEPTH = 2
ALPHA = (2.0 * DEPTH) ** 0.25
LN_EPS = 1e-5
NORM_EPS = 1e-6
NCORES = 8
SEQ_PER_CORE = 4

EPOCH = 30000
NDMA = 8


class _Ins:
    __slots__ = ("fn", "waits", "flag", "dma", "cum")

    def __init__(self, fn, dma=None):
        self.fn = fn
        self.waits = []
        self.flag = False
        self.dma = dma
        self.cum = 0


def _box(ap):
    t = ap.tensor
    dims = ap.ap
    off = int(ap.offset)
    sz = mybir.dt.size(ap.dtype)
    sp = str(ap.space)
    if sp in ("SB", "PSUM"):
        rs, pc = dims[0]
        if rs == 0:
            p0 = 0
            f0 = off
        else:
            p0 = off // rs
            f0 = off - p0 * rs
        ext = 1
        for st, cn in dims[1:]:
            ext += (cn - 1) * abs(st)
        return (t.name, p0, p0 + pc, f0 * sz, (f0 + ext) * sz)
    ext = 1
    for st, cn in dims:
        ext += (cn - 1) * abs(st)
    return (t.name, 0, 1, off * sz, (off + ext) * sz)


class Prog:
    ENG = ["pe", "act", "dve", "pool", "sp"]

    def __init__(self, nc):
        self.nc = nc
        self.streams = {e: [] for e in self.ENG}
        self.track = {}
        self.waited_c = {e: {} for e in self.ENG}
        self.waited_d = {e: {} for e in self.ENG}
        self.dma_k = {e: 0 for e in self.ENG}
        self.out_dmas = []

    def _deps(self, eng, reads, writes):
        toks = set()
        for ap, is_w in [(a, False) for a in reads] + [(a, True) for a in writes]:
            name, p0, p1, f0, f1 = _box(ap)
            recs = self.track.get(name)
            if not recs:
                continue
            keep = []
            for r in recs:
                rp0, rp1, rf0, rf1, rtok, rw = r
                ov = not (rp1 <= p0 or p1 <= rp0 or rf1 <= f0 or f1 <= rf0)
                if ov and (rw or is_w):
                    toks.add(rtok)
                if is_w and ov and rp0 >= p0 and rp1 <= p1 and rf0 >= f0 and rf1 <= f1:
                    continue
                keep.append(r)
            self.track[name] = keep
        return toks

    def _register(self, tok, reads, writes):
        for ap, is_w in [(a, False) for a in reads] + [(a, True) for a in writes]:
            name, p0, p1, f0, f1 = _box(ap)
            recs = self.track.setdefault(name, [])
            if not is_w and tok[0] == "c":
                for i, r in enumerate(recs):
                    if (not r[5]) and r[0] == p0 and r[1] == p1 and r[2] == f0 and r[3] == f1 \
                            and r[4][0] == "c" and r[4][1] == tok[1]:
                        recs[i] = (p0, p1, f0, f1, tok, False)
                        break
                else:
                    recs.append((p0, p1, f0, f1, tok, False))
            else:
                recs.append((p0, p1, f0, f1, tok, is_w))

    def _add_waits(self, eng, ins, toks):
        for tok in toks:
            if tok[0] == "c":
                _, f, idx = tok
                if f == eng and eng == "pe":
                    continue
                if self.waited_c[eng].get(f, -1) >= idx:
                    continue
                self.waited_c[eng][f] = idx
                self.streams[f][idx].flag = True
                ins.waits.append(tok)
            else:
                _, q, k = tok
                key = (q, k % NDMA)
                if self.waited_d[eng].get(key, -1) >= k:
                    continue
                self.waited_d[eng][key] = k
                ins.waits.append(tok)

    def op(self, eng, fn, reads=(), writes=()):
        reads = [a for a in reads if a is not None and not isinstance(a, (int, float))]
        writes = [a for a in writes if a is not None]
        ins = _Ins(fn)
        toks = self._deps(eng, reads, writes)
        self._add_waits(eng, ins, toks)
        idx = len(self.streams[eng])
        self.streams[eng].append(ins)
        self._register(("c", eng, idx), reads, writes)
        return ins

    def dma(self, q, out, in_, is_output=False, **kw):
        k = self.dma_k[q]
        self.dma_k[q] += 1
        ins = _Ins(lambda e: e.dma_start(out=out, in_=in_, **kw), dma=(q, k))
        toks = self._deps(q, [in_], [out])
        if k >= NDMA:
            toks.add(("d", q, k - NDMA))
        self._add_waits(q, ins, toks)
        self.streams[q].append(ins)
        tok = ("d", q, k)
        self._register(tok, [in_], [out])
        if is_output:
            self.out_dmas.append(tok)
        return ins

    def mm(self, out, lhsT, rhs, start=True, stop=True):
        return self.op("pe", lambda e: e.matmul(out, lhsT, rhs, start=start, stop=stop),
                       [lhsT, rhs], [out])

    def transpose(self, out, in_, ident):
        return self.op("pe", lambda e: e.transpose(out, in_, ident), [in_, ident], [out])

    def act(self, out, in_, func, bias=0.0, scale=1.0, accum_out=None):
        kw = {}
        if accum_out is not None:
            kw["accum_out"] = accum_out
        return self.op("act", lambda e: e.activation(out, in_, func, bias=bias, scale=scale, **kw),
                       [in_, bias, scale], [out, accum_out])

    def tt(self, eng, out, in0, in1, op):
        return self.op(eng, lambda e: e.tensor_tensor(out, in0, in1, op), [in0, in1], [out])

    def ts(self, eng, out, in0, s1, s2=None, op0=ALU.mult, op1=None):
        kw = {}
        if op1 is not None:
            kw["op1"] = op1
        return self.op(eng, lambda e: e.tensor_scalar(out, in0, s1, s2, op0, **kw),
                       [in0, s1, s2], [out])

    def stt(self, eng, out, in0, scalar, in1, op0, op1):
        return self.op(eng, lambda e: e.scalar_tensor_tensor(out, in0, scalar, in1, op0, op1),
                       [in0, scalar, in1], [out])

    def copy(self, eng, out, in_):
        if eng == "act":
            return self.op("act", lambda e: e.copy(out, in_), [in_], [out])
        return self.op(eng, lambda e: e.tensor_copy(out, in_), [in_], [out])

    def memset(self, eng, out, val):
        return self.op(eng, lambda e: e.memset(out, val), [], [out])

    def reduce(self, eng, out, in_, op, axis=AX.X):
        return self.op(eng, lambda e: e.tensor_reduce(out, in_, axis, op), [in_], [out])

    def recip(self, out, in_):
        return self.op("dve", lambda e: e.reciprocal(out, in_), [in_], [out])

    def scan(self, out, d0, d1, init, op0, op1):
        return self.op("dve", lambda e: e.tensor_tensor_scan(out, d0, d1, init, op0, op1),
                       [d0, d1, init], [out])

    def emit(self):
        nc = self.nc
        fin = _Ins(lambda e: None)
        self._add_waits("sp", fin, set(self.out_dmas))
        self.streams["sp"].append(fin)
        nsem = {}
        for e in self.ENG:
            cum = 0
            for ins in self.streams[e]:
                if ins.flag:
                    cum += 1
                ins.cum = cum
            nsem[e] = (cum + EPOCH - 1) // EPOCH
        with contextlib.ExitStack() as es:
            csem = {e: [es.enter_context(nc.semaphore(f"c_{e}_{i}")) for i in range(nsem[e])]
                    for e in self.ENG}
            dsem = {e: [es.enter_context(nc.semaphore(f"d_{e}_{i}")) for i in range(NDMA)]
                    for e in self.ENG if self.dma_k[e] > 0}
            block = es.enter_context(nc.Block())
            streams = self.streams

            def run(e, eh):
                for ins in streams[e]:
                    for tok in ins.waits:
                        if tok[0] == "c":
                            c = streams[tok[1]][tok[2]].cum
                            eh.wait_ge(csem[tok[1]][(c - 1) // EPOCH], (c - 1) % EPOCH + 1)
                        else:
                            _, q, k = tok
                            eh.wait_ge(dsem[q][k % NDMA], 16 * (k // NDMA + 1))
                    r = ins.fn(eh)
                    if r is None:
                        continue
                    if ins.dma is not None:
                        q, k = ins.dma
                        r.then_inc(dsem[q][k % NDMA], 16)
                    elif ins.flag:
                        c = ins.cum
                        r.then_inc(csem[e][(c - 1) // EPOCH], 1)

            @block.tensor
            def _(eh):
                run("pe", eh)

            @block.scalar
            def _(eh):
                run("act", eh)

            @block.vector
            def _(eh):
                run("dve", eh)

            @block.gpsimd
            def _(eh):
                run("pool", eh)

            @block.sync
            def _(eh):
                run("sp", eh)


class _Blob:
    def __init__(self):
        self.parts = []
        self.off = {}
        self.n = 0

    def add(self, name, arr):
        a = np.ascontiguousarray(arr, dtype=np.float32).reshape(-1)
        assert a.size % 128 == 0, name
        self.off[name] = (self.n, a.size)
        self.parts.append(a)
        self.n += a.size

    def cat(self):
        return np.concatenate(self.parts)


def _wlayout(inp):
    B = _Blob()
    for l in range(DEPTH):
        for f in (1, 2):
            wg = inp[f"ffn{f}_wg"][l].reshape(8, 128, NFC, 128)
            wu = inp[f"ffn{f}_wu"][l].reshape(8, 128, NFC, 128)
            gu = np.stack([wg, wu], 0)
            B.add(f"gu{l}{f}", gu.transpose(3, 2, 0, 1, 4))
            wd = inp[f"ffn{f}_wd"][l].reshape(NFC, 128, 8, 128)
            B.add(f"wd{l}{f}", wd.transpose(2, 1, 0, 3))
        pg = inp["ple_wg"][l].reshape(8, 128, 8, 128)
        pp = inp["ple_wp"][l].reshape(2, 128, 8, 128)
        B.add(f"ple{l}", np.concatenate([pg, pp], 0).transpose(2, 1, 0, 3))
    wi = inp["ab_w_in"][0].reshape(8, 128, 1792)
    B.add("abq", wi[:, :, 0:512].reshape(8, 128, 8, 64).transpose(2, 1, 0, 3))
    B.add("abk", wi[:, :, 512:640].reshape(8, 128, 2, 64).transpose(2, 1, 0, 3))
    B.add("abv", wi[:, :, 640:768].transpose(1, 0, 2))
    B.add("abx", wi[:, :, 768:1280].reshape(8, 128, 4, 128).transpose(2, 1, 0, 3))
    B.add("abg", wi[:, :, 1280:1792].reshape(8, 128, 4, 128).transpose(2, 1, 0, 3))
    wo = inp["ab_w_out"][0]
    B.add("aboa", wo[:512].reshape(8, 64, 8, 128).transpose(2, 1, 0, 3))
    B.add("abob", wo[512:].reshape(4, 128, 8, 128).transpose(2, 1, 0, 3))
    for nm, key in (("abwa", "b_wa"), ("abwx", "b_wx")):
        w = inp[key][0]
        bd = np.zeros((4, 128, 128), np.float32)
        for c in range(4):
            for hb in range(2):
                bd[c, hb * 64:(hb + 1) * 64, hb * 64:(hb + 1) * 64] = w[2 * c + hb]
        B.add(nm, bd)
    ci = inp["c_w_in"][0].reshape(8, 128, 4112)
    B.add("cw", ci[:, :, :4096].reshape(8, 128, 4, 8, 128).transpose(3, 2, 1, 0, 4))
    B.add("cba", ci[:, :, 4096:4112].transpose(1, 0, 2))
    B.add("cwo", inp["c_w_out"][0].reshape(8, 128, 8, 128))
    return B


def _clayout(inp):
    cols = {}
    parts = []
    n = 0

    def add(name, a):
        nonlocal n
        a = np.ascontiguousarray(a, dtype=np.float32).reshape(128, -1)
        cols[name] = (n, a.shape[1])
        parts.append(a)
        n += a.shape[1]

    add("ln_g", inp["ln_g"].reshape(6, 8, 128).transpose(2, 0, 1))
    add("ln_b", inp["ln_b"].reshape(6, 8, 128).transpose(2, 0, 1))
    add("ple_bg", inp["ple_bg"].reshape(2, 8, 128).transpose(2, 0, 1))
    add("ones", np.ones((128, 128), np.float32))
    add("ident", np.eye(128, dtype=np.float32))
    pidx = np.arange(128)
    sk = inp["a_sinks"][0]
    add("sink", np.stack([sk[2 * hp + pidx // 64] for hp in range(4)], 1))
    slopes = (2.0 ** (-8.0 * np.arange(1, 9, dtype=np.float32) / 8)).astype(np.float32)
    dist = np.abs((pidx % 64)[:, None] + 128 - np.arange(192)[None, :]).astype(np.float32)
    add("abias", np.stack([-(slopes[2 * hp + pidx // 64][:, None] * dist) for hp in range(4)], 1))
    add("convw", inp["b_conv_w"][0].reshape(4, 4, 128).transpose(2, 1, 0))
    add("convb", inp["b_conv_b"][0].reshape(4, 128).T)
    add("bba", inp["b_ba"][0].reshape(4, 128).T)
    add("bbx", inp["b_bx"][0].reshape(4, 128).T)
    add("blam", inp["b_lam"][0].reshape(4, 128).T)
    add("cconv", inp["c_conv_w"][0].reshape(4, 24, 128).transpose(2, 1, 0))
    add("alog", np.broadcast_to(inp["c_a_log"][0][None, :], (128, 8)))
    add("dtb", np.broadcast_to(inp["c_dt_bias"][0][None, :], (128, 8)))
    add("normg", inp["c_norm_g"][0].reshape(128, 1))
    kk = np.arange(128)[:, None]
    ii = np.arange(64)[None, :]
    add("maskU", (kk <= ii).astype(np.float32))
    add("maskneg", np.where(ii >= kk, 0.0, -1e30).astype(np.float32))
    add("nstrict", np.where(ii > kk, -1.0, 0.0).astype(np.float32))
    k2 = np.arange(128)[:, None]
    i2 = np.arange(128)[None, :]
    same = (k2 // 64) == (i2 // 64)
    add("maskUbd", (same & ((k2 % 64) <= (i2 % 64))).astype(np.float32))
    add("masknegbd", np.where(same & ((i2 % 64) >= (k2 % 64)), 0.0, -1e30).astype(np.float32))
    add("nstrictbd", np.where(same & ((i2 % 64) > (k2 % 64)), -1.0, 0.0).astype(np.float32))
    return np.concatenate(parts, 1), cols


def build_program(nseq, woff, wtotal, ccols, nccols, stop_after=None):
    nc = bass.Bass("TRN2", target_bir_lowering=False)
    P = Prog(nc)
    xT = nc.dram_tensor("xT", [nseq, D, S], F32, kind="ExternalInput").ap()
    pT = nc.dram_tensor("pT", [DEPTH, nseq, DPLE, S], F32, kind="ExternalInput").ap()
    wblob = nc.dram_tensor("wblob", [wtotal], F32, kind="ExternalInput").ap()
    cblob = nc.dram_tensor("cblob", [128, nccols], F32, kind="ExternalInput").ap()
    outT = nc.dram_tensor("outT", [nseq, D, S], F32, kind="ExternalOutput").ap()
    wsc = nc.dram_tensor("wsc", [wtotal], BF16, kind="Internal").ap()

    def wview(name, pattern, **kw):
        o, n = woff[name]
        return wsc[o:o + n].rearrange(pattern, **kw)

    with contextlib.ExitStack() as es:
        def sb(name, shape, dt):
            return es.enter_context(nc.sbuf_tensor(name, shape, dt))

        X = sb("X", [128, 8, S], F32)
        XB = sb("XB", [128, 8, S], BF16)
        CST = sb("CST", [128, nccols], F32)
        ONESB = sb("ONESB", [128, 128], BF16)
        WGU = [sb(f"WGU{i}", [128, 2, 8, 128], BF16) for i in range(3)]
        WD = [sb(f"WD{i}", [128, NFC, 128], BF16) for i in range(2)]
        AR_BYTES = 80 * 1024
        ARENA = sb("ARENA", [128, AR_BYTES // 2], BF16)
        PS = es.enter_context(nc.psum_tensor("PS", [128, 8, 512], F32))

        def carve(off_bytes, shape, dt, parts=128):
            n = int(np.prod(shape[1:]))
            szb = 2 if dt == BF16 else 4
            assert off_bytes % 4 == 0 and off_bytes + n * szb <= AR_BYTES, (off_bytes, shape)
            a = ARENA[0:parts, off_bytes // 2: off_bytes // 2 + n * szb // 2]
            if dt == F32:
                a = a.bitcast(F32)
            if len(shape) == 3:
                a = a.rearrange("p (a b) -> p a b", a=shape[1])
            elif len(shape) == 4:
                a = a.rearrange("p (a b c) -> p a b c", a=shape[1], b=shape[2])
            return a

        def cst(name, j=0, n=1):
            o, w = ccols[name]
            return CST[:, o + j:o + j + n]

        P.dma("sp", CST[:], cblob)
        o, w = ccols["ones"]
        P.copy("dve", ONESB[:], CST[:, o:o + 128])
        CH = 1 << 21
        pos = 0
        while pos < wtotal:
            n = min(CH, wtotal - pos)
            P.dma("pool", wsc[pos:pos + n].rearrange("(p f) -> p f", p=128),
                  wblob[pos:pos + n].rearrange("(p f) -> p f", p=128))
            pos += n

        wslot = [0]

        def pump(gen, n=1):
            if gen is None:
                return None
            for _ in range(n):
                try:
                    next(gen)
                except StopIteration:
                    return None
            return gen

        def drain(gen):
            while gen is not None:
                gen = pump(gen, 64)

        def layer_norm_gen(t0, nt, l, i, eps):
            for tt in range(nt):
                ts_ = slice(t0 + tt * 512, t0 + (tt + 1) * 512)
                RB = carve(64 * 1024, [128, 8, 512], BF16)
                RQ = carve(72 * 1024, [128, 8, 512], BF16)
                xs = X[:, :, ts_]
                for hh in range(2):
                    P.act(RB[:, hh * 4:(hh + 1) * 4, :], X[:, hh * 4:(hh + 1) * 4, ts_], AF.Copy)
                    yield
                    P.act(RQ[:, hh * 4:(hh + 1) * 4, :], X[:, hh * 4:(hh + 1) * 4, ts_], AF.Square)
                    yield
                for dc in range(8):
                    P.mm(PS[:, 6, :], ONESB[:], RB[:, dc, :], start=(dc == 0), stop=(dc == 7))
                for dc in range(8):
                    P.mm(PS[:, 7, :], ONESB[:], RQ[:, dc, :], start=(dc == 0), stop=(dc == 7))
                yield
                ST = carve(44 * 1024 + 2048 + (tt % 2) * 6144, [128, 3, 512], F32)
                M, V, R = ST[:, 0, :], ST[:, 1, :], ST[:, 2, :]
                P.ts("dve", M, PS[:, 6, :], 1.0 / D, None, op0=ALU.mult)
                P.tt("dve", V, M, M, ALU.mult)
                P.stt("dve", V, PS[:, 7, :], 1.0 / D, V, ALU.mult, ALU.subtract)
                P.ts("dve", V, V, eps, None, op0=ALU.add)
                P.act(R, V, AF.Sqrt)
                P.recip(R, R)
                yield
                for hh in range(2):
                    xh = X[:, hh * 4:(hh + 1) * 4, ts_]
                    P.tt("dve", xh, xh, M.unsqueeze(1).to_broadcast([128, 4, 512]), ALU.subtract)
                    yield
                    P.tt("dve", xh, xh, R.unsqueeze(1).to_broadcast([128, 4, 512]), ALU.mult)
                    yield
                for dc in range(8):
                    xd = X[:, dc, ts_]
                    gcol = cst("ln_g", (l * 3 + i) * 8 + dc)
                    bcol = cst("ln_b", (l * 3 + i) * 8 + dc)
                    P.act(XB[:, dc, ts_], xd, AF.Identity, bias=bcol, scale=gcol)
                    P.act(xd, xd, AF.Identity, bias=bcol, scale=gcol)
                    if dc % 2:
                        yield

        def layer_norm(t0, nt, l, i, eps):
            drain(layer_norm_gen(t0, nt, l, i, eps))

        def ffn(l, f, lni, bg=None):
            gu = wview(f"gu{l}{f}", "(fc p r) -> fc p r", fc=NFC, p=128)
            wd = wview(f"wd{l}{f}", "(dc p r) -> dc p r", dc=8, p=128)
            H = carve(0, [128, NFC, 1024], BF16)
            SG = carve(44 * 1024, [128, 2, 512], BF16)
            for half in range(2):
                t0 = half * 1024
                for fc in range(NFC):
                    wb = WGU[wslot[0] % 3]
                    wslot[0] += 1
                    P.dma("sp", wb[:].rearrange("p a b c -> p (a b c)"), gu[fc])
                    for sub in range(2):
                        tsl = slice(t0 + sub * 512, t0 + (sub + 1) * 512)
                        bg_, bu = (0, 1) if (fc * 2 + sub) % 2 == 0 else (2, 3)
                        for kc in range(8):
                            P.mm(PS[:, bg_, :], wb[:, 0, kc, :], XB[:, kc, tsl], start=(kc == 0), stop=(kc == 7))
                        for kc in range(8):
                            P.mm(PS[:, bu, :], wb[:, 1, kc, :], XB[:, kc, tsl], start=(kc == 0), stop=(kc == 7))
                        sg = SG[:, (fc * 2 + sub) % 2, :]
                        P.act(sg, PS[:, bg_, :], AF.Silu)
                        P.tt("dve", H[:, fc, sub * 512:(sub + 1) * 512], sg, PS[:, bu, :], ALU.mult)
                        bg = pump(bg, 2)
                drain(bg)
                for dc in range(8):
                    wdb = WD[dc % 2]
                    P.dma("sp", wdb[:].rearrange("p a b -> p (a b)"), wd[dc])
                    for sub in range(2):
                        tsl = slice(t0 + sub * 512, t0 + (sub + 1) * 512)
                        bank = 4 + (dc * 2 + sub) % 2
                        for fc in range(NFC):
                            P.mm(PS[:, bank, :], wdb[:, fc, :], H[:, fc, sub * 512:(sub + 1) * 512],
                                 start=(fc == 0), stop=(fc == NFC - 1))
                        xs = X[:, dc, tsl]
                        P.stt("dve", xs, PS[:, bank, :], 0.5 / ALPHA, xs, ALU.mult, ALU.add)
                bg = layer_norm_gen(t0, 2, l, lni, LN_EPS / (ALPHA * ALPHA))
            return bg

        def ple(l, s, bg=None):
            pw = wview(f"ple{l}", "(dc p r) -> p dc r", dc=8, p=128)
            PT = carve(0, [128, 2, S], BF16)
            for kc in range(2):
                P.dma("pool", PT[:, kc, :], pT[l, s, kc * 128:(kc + 1) * 128, :])
            SGM = carve(8 * 1024, [128, 2, 512], F32)
            WP = carve(12 * 1024, [128, 8, 10, 128], BF16)
            for dc in range(8):
                P.dma("sp", WP[:, dc].rearrange("p a b -> p (a b)"), pw[:, dc, :])
            for tt in range(4):
                tsl = slice(tt * 512, (tt + 1) * 512)
                if tt == 2:
                    drain(bg)
                    bg = None
                for dc in range(8):
                    wb = WP[:, dc]
                    i2 = (dc * 4 + tt) % 2
                    for kc in range(8):
                        P.mm(PS[:, i2, :], wb[:, kc, :], XB[:, kc, tsl], start=(kc == 0), stop=(kc == 7))
                    for kc in range(2):
                        P.mm(PS[:, 2 + i2, :], wb[:, 8 + kc, :], PT[:, kc, tsl], start=(kc == 0), stop=(kc == 1))
                    sg = SGM[:, i2, :]
                    P.act(sg, PS[:, i2, :], AF.Sigmoid, bias=cst("ple_bg", l * 8 + dc))
                    P.tt("dve", sg, sg, PS[:, 2 + i2, :], ALU.mult)
                    P.tt("dve", X[:, dc, tsl], X[:, dc, tsl], sg, ALU.add)
                    bg = pump(bg, 2)
                for hh in range(2):
                    P.copy("act" if hh else "dve", XB[:, hh * 4:(hh + 1) * 4, tsl], X[:, hh * 4:(hh + 1) * 4, tsl])

        IDB = sb("IDB", [128, 128], BF16)
        o_, w_ = ccols["ident"]
        P.copy("dve", IDB[:], CST[:, o_:o_ + 128])
        C0 = sb("C0", [128, 8], F32)
        P.act(C0[:, 0:4], cst("blam", 0, 4), AF.Exp, scale=-1.0)
        P.act(C0[:, 0:4], C0[:, 0:4], AF.Ln, bias=1.0)
        P.ts("dve", C0[:, 4:8], C0[:, 0:4], -16.0, None, op0=ALU.mult)
        P.ts("dve", C0[:, 0:4], C0[:, 0:4], -8.0, None, op0=ALU.mult)

        def wload(view, shape_free):
            wb = WGU[wslot[0] % 3]
            wslot[0] += 1
            n = int(np.prod(shape_free))
            flat = wb[:].rearrange("p a b c -> p (a b c)")
            np_ = view.shape[0]
            P.dma("sp", flat[0:np_, 0:n], view)
            return flat[0:np_, 0:n]

        def resid_add(dc, tsl, bank, scale):
            xs = X[:, dc, tsl]
            P.stt("dve", xs, PS[:, bank, :], scale, xs, ALU.mult, ALU.add)

        def mixer_ab(s, bg=None):
            bankc = [0]

            def nb(lo=0, hi=6):
                b = lo + bankc[0] % (hi - lo)
                bankc[0] += 1
                return b

            KT = carve(0, [64, 2, S], BF16, parts=64)
            V = carve(8 * 1024, [64, 32, 128], BF16, parts=64)
            QT = carve(16 * 1024, [64, 16 * 512], BF16, parts=64)
            QTv = QT.rearrange("p (n hp h2 c) -> p n hp h2 c", n=16, hp=4, h2=2)
            YAT = carve(32 * 1024, [64, 8, 1024], BF16, parts=64)
            WK0 = 48 * 1024

            def bc_last(ap, n):
                return ap.unsqueeze(len(ap.shape)).to_broadcast(list(ap.shape) + [n])

            wk = wview("abk", "(kv p r) -> kv p r", kv=2, p=128)
            wks = [wload(wk[kv], [8, 64]).rearrange("p (a b) -> p a b", a=8) for kv in range(2)]
            wv = wload(wview("abv", "(p r) -> p r", p=128), [8, 128]).rearrange("p (a b) -> p a b", a=8)
            for tt in range(4):
                if tt == 2:
                    drain(bg)
                    bg = None
                for kv in range(2):
                    tsl = slice(tt * 512, (tt + 1) * 512)
                    b = nb(0, 4)
                    for kc in range(8):
                        P.mm(PS[0:64, b, :], wks[kv][:, kc, :], XB[:, kc, tsl], start=(kc == 0), stop=(kc == 7))
                    P.copy("act", KT[:, kv, tsl], PS[0:64, b, :])
                    bg = pump(bg, 6)
            for g in range(8):
                if g < 4:
                    bg = pump(bg, 6)
                else:
                    drain(bg)
                    bg = None
                b = nb(0, 4)
                for cc in range(4):
                    n = g * 4 + cc
                    for kc in range(8):
                        P.mm(PS[0:64, b, cc * 128:(cc + 1) * 128], XB[:, kc, n * 64:(n + 1) * 64], wv[:, kc, :],
                             start=(kc == 0), stop=(kc == 7))
                P.copy("act", V[:, g * 4:(g + 1) * 4, :], PS[0:64, b, :].rearrange("p (a b) -> p a b", a=4))
            wq = wview("abq", "(h p r) -> h p r", h=8, p=128)
            woa = wview("aboa", "(dc p r) -> dc p r", dc=8, p=64)
            PSB = PS[:, 4:6, :].bitcast(BF16).rearrange("p a b -> p (a b)")
            o_, w_ = ccols["abias"]
            ABIAS = CST[:, o_:o_ + 768].rearrange("p (h k) -> p h k", h=4)
            o_, w_ = ccols["sink"]
            SINK4 = CST[:, o_:o_ + 4]
            NSINK4 = carve(WK0 + 15360 + 128, [128, 4], F32)
            P.ts("dve", NSINK4[:], SINK4, -1.0, None, op0=ALU.mult)
            for half in range(2):
                T0 = half * 1024
                for h in range(8):
                    w = wload(wq[h], [8, 64]).rearrange("p (a b) -> p a b", a=8)
                    for t2 in range(2):
                        tsl = slice(T0 + t2 * 512, T0 + (t2 + 1) * 512)
                        b = nb(0, 4)
                        for kc in range(8):
                            P.mm(PS[0:64, b, :], w[:, kc, :], XB[:, kc, tsl], start=(kc == 0), stop=(kc == 7))
                        P.act(QTv[:, t2 * 8:(t2 + 1) * 8, h // 2, h % 2, :],
                              PS[0:64, b, :].rearrange("p (n c) -> p n c", n=8), AF.Copy, scale=0.125)
                def geom(nl):
                    n = half * 16 + nl
                    j0 = max(0, 2 - n)
                    nj = 3 - j0
                    i2 = n % 2
                    SBf = carve(WK0 + i2 * 3072, [128, 2, 2, 192], F32)
                    PN = carve(WK0 + 6144 + i2 * 1536, [128, 4, 192], BF16)
                    PTs = carve(WK0 + 9216 + i2 * 3072, [64, 4, 3, 128], BF16, parts=64)
                    STt = carve(WK0 + 15360 + i2 * 64, [128, 16], F32)
                    return n, j0, nj, nj * 64, (n - 2 + j0) * 64, i2, SBf, SBf.rearrange("p b q k -> p (b q) k"), PN, PTs, STt

                def stage_a1(nl):
                    n, j0, nj, nk, k0, i2, SBf, SB4, PN, PTs, STt = geom(nl)
                    for hp in range(4):
                        P.mm(PS[:, 2 * i2 + hp // 2, (hp % 2) * 192:(hp % 2) * 192 + nk],
                             QTv[:, nl, hp].rearrange("p a b -> p (a b)"), KT[:, hp // 2, k0:k0 + nk])
                    S4 = PS[:, 2 * i2:2 * i2 + 2, 0:384].rearrange("p b (q k) -> p b q k", q=2)
                    P.tt("dve", SBf[:, :, :, 0:nk], S4[:, :, :, 0:nk],
                         ABIAS[:, :, j0 * 64:192].rearrange("p (b q) k -> p b q k", b=2), ALU.add)
                    P.reduce("dve", STt[:, 0:4], SB4[:, :, 0:nk], ALU.max)
                    P.stt("dve", STt[:, 4:8], STt[:, 0:4], -1.0, NSINK4[:], ALU.mult, ALU.min)
                    P.memset("dve", STt[:, 8:12], 0.0)
                    P.tt("dve", STt[:, 12:16], SINK4, STt[:, 4:8], ALU.add)
                    for hp in range(4):
                        P.act(SB4[:, hp, 0:nk], SB4[:, hp, 0:nk], AF.Exp, bias=STt[:, 4 + hp:5 + hp],
                              accum_out=STt[:, 8 + hp:9 + hp])
                    P.act(STt[:, 12:16], STt[:, 12:16], AF.Exp)

                def stage_a2(nl):
                    n, j0, nj, nk, k0, i2, SBf, SB4, PN, PTs, STt = geom(nl)
                    P.tt("dve", STt[:, 8:12], STt[:, 8:12], STt[:, 12:16], ALU.add)
                    P.recip(STt[:, 8:12], STt[:, 8:12])
                    P.tt("dve", PN[:, :, 0:nk], SB4[:, :, 0:nk], bc_last(STt[:, 8:12], nk), ALU.mult)

                def stage_b(nl):
                    n, j0, nj, nk, k0, i2, SBf, SB4, PN, PTs, STt = geom(nl)
                    for hp in range(4):
                        for jj in range(nj):
                            idx = hp * 3 + jj
                            P.transpose(PSB[0:64, idx * 128:(idx + 1) * 128], PN[:, hp, jj * 64:(jj + 1) * 64], IDB[:])
                    P.copy("act", PTs[:, :, 0:nj, :],
                           PSB[0:64, 0:1536].rearrange("p (h j c) -> p h j c", h=4, j=3)[:, :, 0:nj, :])
                    for hp in range(4):
                        for jj in range(nj):
                            P.mm(PS[0:64, 6 + i2, hp * 128:(hp + 1) * 128],
                                 V[:, n - 2 + j0 + jj, (hp // 2) * 64:(hp // 2 + 1) * 64], PTs[:, hp, jj, :],
                                 start=(jj == 0), stop=(jj == nj - 1))
                    P.copy("act", YAT[:, :, nl * 64:(nl + 1) * 64],
                           PS[0:64, 6 + i2, :].rearrange("p (h c) -> p h c", h=8))

                stage_a1(0)
                for nl in range(16):
                    if nl + 1 < 16:
                        stage_a1(nl + 1)
                    stage_a2(nl)
                    stage_b(nl)
                for dc in range(8):
                    w = wload(woa[dc], [8, 128]).rearrange("p (a b) -> p a b", a=8)
                    for t2 in range(2):
                        tsl = slice(T0 + t2 * 512, T0 + (t2 + 1) * 512)
                        b = nb(0, 4)
                        for h in range(8):
                            P.mm(PS[:, b, :], w[:, h, :], YAT[:, h, t2 * 512:(t2 + 1) * 512], start=(h == 0), stop=(h == 7))
                        resid_add(dc, tsl, b, 1.0 / ALPHA)
            BXP = carve(0, [128, 2052], F32)
            BXC = carve(8208, [128, S], F32)
            RG = carve(16400, [128, S], F32)
            IG = carve(24592, [128, S], F32)
            GG = carve(32784, [128, S], F32)
            AA = carve(40976, [128, S], F32)
            BXB = carve(49168, [128, S], BF16)
            YBT = carve(53264, [128, 4, S], BF16)
            wx = wview("abx", "(c p r) -> c p r", c=4, p=128)
            wg_ = wview("abg", "(c p r) -> c p r", c=4, p=128)
            wa_ = wview("abwa", "(c p r) -> c p r", c=4, p=128)
            wxx = wview("abwx", "(c p r) -> c p r", c=4, p=128)
            P.memset("pool", BXP[:, 0:3], 0.0)
            for c in range(4):
                w = wload(wx[c], [8, 128]).rearrange("p (a b) -> p a b", a=8)
                for tt in range(4):
                    tsl = slice(tt * 512, (tt + 1) * 512)
                    b = nb()
                    for kc in range(8):
                        P.mm(PS[:, b, :], w[:, kc, :], XB[:, kc, tsl], start=(kc == 0), stop=(kc == 7))
                    P.copy("act", BXP[:, 3 + tt * 512:3 + (tt + 1) * 512], PS[:, b, :])
                w = wload(wg_[c], [8, 128]).rearrange("p (a b) -> p a b", a=8)
                for tt in range(4):
                    tsl = slice(tt * 512, (tt + 1) * 512)
                    b = nb()
                    for kc in range(8):
                        P.mm(PS[:, b, :], w[:, kc, :], XB[:, kc, tsl], start=(kc == 0), stop=(kc == 7))
                    P.act(GG[:, tsl], PS[:, b, :], AF.Gelu)
                o_, w_ = ccols["convw"]
                cw = lambda j: CST[:, o_ + c * 4 + j:o_ + c * 4 + j + 1]
                P.ts("dve", BXC[:], BXP[:, 0:S], cw(0), cst("convb", c), op0=ALU.mult, op1=ALU.add)
                P.stt("dve", BXC[:], BXP[:, 1:1 + S], cw(1), BXC[:], ALU.mult, ALU.add)
                P.stt("dve", BXC[:], BXP[:, 2:2 + S], cw(2), BXC[:], ALU.mult, ALU.add)
                P.stt("dve", BXC[:], BXP[:, 3:3 + S], cw(3), BXC[:], ALU.mult, ALU.add)
                P.copy("act", BXB[:], BXC[:])
                wa = wload(wa_[c], [128])
                wx2 = wload(wxx[c], [128])
                for tt in range(4):
                    tsl = slice(tt * 512, (tt + 1) * 512)
                    b = nb()
                    P.mm(PS[:, b, :], wa, BXB[:, tsl])
                    P.act(RG[:, tsl], PS[:, b, :], AF.Sigmoid, bias=cst("bba", c))
                    b = nb()
                    P.mm(PS[:, b, :], wx2, BXB[:, tsl])
                    P.act(IG[:, tsl], PS[:, b, :], AF.Sigmoid, bias=cst("bbx", c))
                P.act(AA[:], RG[:], AF.Exp, scale=C0[:, c:c + 1])
                P.act(RG[:], RG[:], AF.Exp, scale=C0[:, 4 + c:5 + c])
                P.act(RG[:], RG[:], AF.Sqrt, bias=1.0, scale=-1.0)
                P.tt("dve", IG[:], IG[:], BXC[:], ALU.mult)
                P.tt("dve", IG[:], IG[:], RG[:], ALU.mult)
                P.scan(RG[:], AA[:], IG[:], 0.0, ALU.mult, ALU.add)
                P.tt("dve", YBT[:, c, :], RG[:], GG[:], ALU.mult)
            wob = wview("abob", "(dc p r) -> dc p r", dc=8, p=128)
            for dc in range(8):
                w = wload(wob[dc], [4, 128]).rearrange("p (a b) -> p a b", a=4)
                for tt in range(4):
                    tsl = slice(tt * 512, (tt + 1) * 512)
                    b = nb()
                    for c in range(4):
                        P.mm(PS[:, b, :], w[:, c, :], YBT[:, c, tsl], start=(c == 0), stop=(c == 3))
                    resid_add(dc, tsl, b, 1.0 / ALPHA)
            drain(layer_norm_gen(0, 2, 0, 1, LN_EPS / (ALPHA * ALPHA)))
            return layer_norm_gen(1024, 2, 0, 1, LN_EPS / (ALPHA * ALPHA))

        NEGA = sb("NEGA", [128, 8], F32)
        P.act(NEGA[:], cst("alog", 0, 8), AF.Exp)
        P.ts("dve", NEGA[:], NEGA[:], -1.0, None, op0=ALU.mult)

        def mixer_c(s, bg=None):
            bankc = [0]

            def nb(lo=0, hi=6):
                b = lo + bankc[0] % (hi - lo)
                bankc[0] += 1
                return b

            def bc_last(ap, n):
                return ap.unsqueeze(len(ap.shape)).to_broadcast(list(ap.shape) + [n])

            def bc_mid(ap, n):
                return ap.unsqueeze(1).to_broadcast([ap.shape[0], n, ap.shape[1]])

            GBR = carve(0, [128, 16, 16], F32)
            BETA = carve(1024, [128, 16, 8], F32)
            GNEG = carve(1536, [128, 16, 8], F32)
            CIN = carve(4096, [128, 2052], F32)
            DECT = carve(4112, [128, 16, 128], F32)
            QN = carve(12304, [128, S], F32)
            KN = carve(20496, [128, S], F32)
            KGT = carve(28688, [128, 16, 128], BF16)
            NWT = carve(61456, [128, S], BF16)
            EG = carve(75792 + 1664, [128, 16], F32)
            QG = carve(28688 + 4096, [128, S], BF16)
            GCB = carve(36880, [128, 32, 64], F32)
            OT = carve(36880, [128, S], F32)
            KDEC = carve(45072, [128, 16, 128], BF16)
            VTOK = carve(49168, [128, 16, 128], BF16)
            ATT = carve(53264, [128, 16, 128], BF16)
            ZB = carve(57360, [128, 16, 128], BF16)
            GU = carve(61456, [128, 16, 128], F32)
            XI = carve(69648, [128, 4, 128], F32)
            XTI = carve(71696, [128, 4, 128], F32)
            PM = carve(73744, [128, 4, 128], F32)
            TMPA = carve(69648, [128, 512], F32)
            TMPB = carve(71696, [128, 512], F32)
            SM0 = 75792
            GL = carve(SM0, [128, 32], F32)
            GCC = carve(SM0 + 128, [128, 16], F32)
            GH = carve(SM0 + 192, [128, 16], F32)
            BH = carve(SM0 + 256, [128, 16], F32)
            S2 = carve(SM0 + 320, [128, 16], F32)
            SF = carve(SM0 + 384, [128, 128], F32)
            SBb = carve(SM0 + 896, [128, 128], BF16)
            RB = carve(SM0 + 1152, [128, 128], BF16)
            VN = carve(SM0 + 1408, [128, 128], BF16)
            SQB = carve(28688 + 4096, [128, S], BF16)
            OGB = carve(28688, [128, S], BF16)
            VSB = WD[0][:].rearrange("p a b -> p (a b)")[:, 0:S]

            o_, w_ = ccols["ones"]
            ONESF = CST[:, o_:o_ + 128]
            o_, w_ = ccols["ident"]
            IDF = CST[:, o_:o_ + 128]
            o_, w_ = ccols["maskUbd"]
            MU = CST[:, o_:o_ + 128]
            o_, w_ = ccols["masknegbd"]
            MNEG = CST[:, o_:o_ + 128]
            o_, w_ = ccols["nstrictbd"]
            NSTR = CST[:, o_:o_ + 128]

            drain(bg)
            wba = wload(wview("cba", "(p r) -> p r", p=128), [8, 16]).rearrange("p (a b) -> p a b", a=8)
            b = nb()
            for pr in range(16):
                for kc in range(8):
                    P.mm(PS[:, b, pr * 16:(pr + 1) * 16], XB[:, kc, pr * 128:(pr + 1) * 128], wba[:, kc, :],
                         start=(kc == 0), stop=(kc == 7))
            P.copy("act", GBR[:], PS[:, b, 0:256].rearrange("p (a b) -> p a b", a=16))
            P.act(BETA[:], GBR[:, :, 0:8], AF.Sigmoid)
            o_, w_ = ccols["dtb"]
            P.tt("dve", GNEG[:], GBR[:, :, 8:16], bc_mid(CST[:, o_:o_ + 8], 16), ALU.add)
            P.act(GNEG[:], GNEG[:], AF.Exp)
            P.act(GNEG[:], GNEG[:], AF.Ln, bias=1.0)
            P.tt("dve", GNEG[:], GNEG[:], bc_mid(NEGA[:, :], 16), ALU.mult)

            cwv = wview("cw", "(h g p r) -> h g p r", h=8, g=4, p=128)
            cwo = wview("cwo", "(h p r) -> h p r", h=8, p=128)
            oc_, w_ = ccols["cconv"]

            CINB = carve(4096, [128, 2052], BF16)
            DG = carve(4096 + 4104, [128, 4, 128], BF16)
            o_, w_ = ccols["ident"]
            IDFc = CST[:, o_:o_ + 128]

            def phaseA(h):
                P.memset("pool", CINB[:, 0:3], 0.0)
                for g, dest in ((0, QN), (1, KN), (2, None)):
                    w = wload(cwv[h, g], [8, 128]).rearrange("p (a b) -> p a b", a=8)
                    cw = lambda j: CST[:, oc_ + (g * 8 + h) * 4 + j: oc_ + (g * 8 + h) * 4 + j + 1]
                    for j in range(4):
                        P.ts("pool", DG[:, j, :], IDFc, cw(j), None, op0=ALU.mult)
                    for tt in range(4):
                        tsl = slice(tt * 512, (tt + 1) * 512)
                        b = nb()
                        for kc in range(8):
                            P.mm(PS[:, b, :], w[:, kc, :], XB[:, kc, tsl], start=(kc == 0), stop=(kc == 7))
                        P.copy("act", CINB[:, 3 + tt * 512:3 + (tt + 1) * 512], PS[:, b, :])
                        yield
                    SQ = carve(69648, [128, S], BF16)
                    for tt in range(4):
                        tsl = slice(tt * 512, (tt + 1) * 512)
                        b = nb()
                        for j in range(4):
                            P.mm(PS[:, b, :], DG[:, j, :], CINB[:, j + tt * 512:j + tt * 512 + 512],
                                 start=(j == 0), stop=(j == 3))
                        if g < 2:
                            P.act(dest[:, tsl], PS[:, b, :], AF.Silu)
                            P.act(SQ[:, tsl], dest[:, tsl], AF.Square)
                        else:
                            P.act(VSB[:, tsl], PS[:, b, :], AF.Silu)
                        yield
                    if g < 2:
                        for tt in range(4):
                            tsl = slice(tt * 512, (tt + 1) * 512)
                            b = nb()
                            P.mm(PS[:, b, :], ONESB[:], SQ[:, tsl])
                            rn = carve(73744, [128, 512], F32)
                            sc = 128.0 if g == 0 else 1.0
                            P.act(rn, PS[:, b, :], AF.Ln, bias=NORM_EPS * sc, scale=sc)
                            P.act(rn, rn, AF.Exp, scale=-0.5)
                            P.tt("pool", dest[:, tsl], dest[:, tsl], rn, ALU.mult)
                            yield

            gen = phaseA(0)
            drain(gen)
            for h in range(8):
                P.copy("pool", GH[:], GNEG[:, :, h])
                P.copy("pool", BH[:], BETA[:, :, h])
                P.tt("dve", GU[:], bc_mid(MU, 16), bc_last(GH[:], 128), ALU.mult)
                for q4 in range(4):
                    b = nb()
                    P.mm(PS[:, b, :], ONESF, GU[:, q4 * 4:(q4 + 1) * 4, :].rearrange("p a b -> p (a b)"))
                    P.copy("act", GCB[:, q4 * 8:(q4 + 1) * 8, :].rearrange("p a b -> p (a b)"), PS[:, b, :])
                b = nb()
                P.mm(PS[:, b, 0:16], MU, GH[:])
                P.copy("act", GCC[:], PS[:, b, 0:16])
                P.act(GL[:], GCB[:, :, 63], AF.Exp)
                GCBv = GCB[:].rearrange("p (a b) c -> p a b c", b=2)
                for par in range(2):
                    rows = slice(par * 64, (par + 1) * 64)
                    P.tt("dve", S2[rows, :], GCBv[rows, :, par, 63], GCC[rows, :], ALU.subtract)
                P.act(S2[:], S2[:], AF.Exp)
                P.act(EG[:], GCC[:], AF.Exp)
                for g4 in range(4):
                    b = nb()
                    for cc in range(4):
                        pr = g4 * 4 + cc
                        P.transpose(PS[:, b, cc * 128:(cc + 1) * 128], KN[:, pr * 128:(pr + 1) * 128], IDF)
                    P.tt("dve", KDEC[:, g4 * 4:(g4 + 1) * 4, :], PS[:, b, :].rearrange("p (a b) -> p a b", a=4),
                         bc_last(S2[:, g4 * 4:(g4 + 1) * 4], 128), ALU.mult)
                    P.tt("dve", KGT[:, g4 * 4:(g4 + 1) * 4, :], PS[:, b, :].rearrange("p (a b) -> p a b", a=4),
                         bc_last(EG[:, g4 * 4:(g4 + 1) * 4], 128), ALU.mult)
                    b = nb()
                    PSv = PS[:, b, :].bitcast(BF16)
                    for cc in range(4):
                        pr = g4 * 4 + cc
                        P.transpose(PSv[:, cc * 128:(cc + 1) * 128], VSB[:, pr * 128:(pr + 1) * 128], IDB[:])
                    P.copy("act", VTOK[:, g4 * 4:(g4 + 1) * 4, :], PSv[:, 0:512].rearrange("p (a b) -> p a b", a=4))
                GCBp = GCB[:].rearrange("p (a b) c -> p a (b c)", b=2)
                P.tt("dve", DECT[:], GCBp, bc_last(GCC[:], 128), ALU.subtract)
                P.tt("dve", DECT[:], DECT[:], bc_mid(MNEG, 16), ALU.add)
                P.act(DECT[:], DECT[:], AF.Exp)
                GCBf = GCB[:].rearrange("p a b -> p (a b)")
                P.act(GCBf, GCBf, AF.Exp)
                P.tt("dve", QG[:], QN[:], GCBf, ALU.mult)
                for q4 in range(4):
                    b = nb()
                    for cc in range(4):
                        pr = q4 * 4 + cc
                        P.mm(PS[:, b, cc * 128:(cc + 1) * 128], KN[:, pr * 128:(pr + 1) * 128], QN[:, pr * 128:(pr + 1) * 128])
                    P.tt("dve", ATT[:, q4 * 4:(q4 + 1) * 4, :], PS[:, b, :].rearrange("p (a b) -> p a b", a=4),
                         DECT[:, q4 * 4:(q4 + 1) * 4, :], ALU.mult)
                P.tt("dve", DECT[:], DECT[:], bc_mid(NSTR, 16), ALU.mult)
                P.tt("dve", DECT[:], DECT[:], bc_last(BH[:], 128), ALU.mult)
                bufs = [(XI, XTI, PM),
                        (carve(61456, [128, 4, 128], F32), carve(63504, [128, 4, 128], F32), carve(65552, [128, 4, 128], F32))]
                v4 = lambda b: PS[:, b, :].rearrange("p (a b) -> p a b", a=4)
                for qq in range(2):
                    grp = [(2 * qq + i, bufs[i]) for i in range(2)]
                    for q4, (xi, xti, pm) in grp:
                        b = nb()
                        for cc in range(4):
                            pr = q4 * 4 + cc
                            P.mm(PS[:, b, cc * 128:(cc + 1) * 128], KN[:, pr * 128:(pr + 1) * 128], KN[:, pr * 128:(pr + 1) * 128])
                        P.tt("dve", xi[:], v4(b), DECT[:, q4 * 4:(q4 + 1) * 4, :], ALU.mult)
                    for q4, (xi, xti, pm) in grp:
                        b = nb()
                        for cc in range(4):
                            P.transpose(PS[:, b, cc * 128:(cc + 1) * 128], xi[:, cc, :], IDF)
                        P.copy("act", xti[:], v4(b))
                        P.tt("dve", pm[:], xi[:], bc_mid(IDF, 4), ALU.add)
                    for it in range(5):
                        bb = {}
                        for q4, (xi, xti, pm) in grp:
                            b1 = nb()
                            for cc in range(4):
                                P.mm(PS[:, b1, cc * 128:(cc + 1) * 128], xti[:, cc, :], xi[:, cc, :])
                            b2 = nb()
                            for cc in range(4):
                                P.mm(PS[:, b2, cc * 128:(cc + 1) * 128], xi[:, cc, :], xti[:, cc, :])
                            bb[q4] = (b1, b2)
                        for q4, (xi, xti, pm) in grp:
                            b1, b2 = bb[q4]
                            P.copy("act", xi[:], v4(b1))
                            P.copy("dve", xti[:], v4(b2))
                        for q4, (xi, xti, pm) in grp:
                            b3 = nb()
                            for cc in range(4):
                                P.mm(PS[:, b3, cc * 128:(cc + 1) * 128], xti[:, cc, :], pm[:, cc, :])
                            bb[q4] = b3
                        for q4, (xi, xti, pm) in grp:
                            P.tt("dve", pm[:], pm[:], v4(bb[q4]), ALU.add)
                    for q4, (xi, xti, pm) in grp:
                        P.copy("act", ZB[:, q4 * 4:(q4 + 1) * 4, :], pm[:])
                for g4 in range(4):
                    b = nb()
                    for cc in range(4):
                        pr = g4 * 4 + cc
                        P.mm(PS[:, b, cc * 128:(cc + 1) * 128], KGT[:, pr, :], ZB[:, pr, :])
                    P.act(NWT[:, g4 * 512:(g4 + 1) * 512], PS[:, b, :], AF.Copy, scale=-1.0)
                P.memset("pool", SF[:], 0.0)
                P.memset("pool", SBb[:], 0.0)
                gen = phaseA(h + 1) if h < 7 else None
                for c in range(32):
                    gen = pump(gen, 2)
                    cc = c % 8
                    pr = c // 2
                    pb = (c % 2) * 64
                    rows = slice(pb, pb + 64)
                    csl = slice(c * 64, (c + 1) * 64)
                    ob = 6 + (c // 8) % 2
                    b1 = nb()
                    P.mm(PS[rows, b1, 0:128], ZB[rows, pr, pb:pb + 64], VTOK[rows, pr, :], start=True, stop=False)
                    P.mm(PS[rows, b1, 0:128], NWT[:, csl], SBb[:], start=False, stop=True)
                    P.ts("dve", VN[rows, :], PS[rows, b1, 0:128], BH[rows, pr:pr + 1], None, op0=ALU.mult)
                    P.mm(PS[:, ob, cc * 64:(cc + 1) * 64], SBb[:], QG[:, csl], start=True, stop=False)
                    P.mm(PS[:, ob, cc * 64:(cc + 1) * 64], VN[rows, :], ATT[rows, pr, pb:pb + 64], start=False, stop=True)
                    b3 = nb()
                    P.mm(PS[:, b3, 0:128], KDEC[rows, pr, :], VN[rows, :])
                    P.stt("dve", SF[:], SF[:], GL[:, c:c + 1], PS[:, b3, 0:128], ALU.mult, ALU.add)
                    P.copy("dve", SBb[:], SF[:])
                    if cc == 7:
                        P.copy("act", OT[:, (c - 7) * 64:(c + 1) * 64], PS[:, ob, :])
                drain(gen)
                P.act(SQB[:], OT[:], AF.Square)
                wz = wload(cwv[h, 3], [8, 128]).rearrange("p (a b) -> p a b", a=8)
                wo = wload(cwo[h], [8, 128]).rearrange("p (a b) -> p a b", a=8)
                for tt in range(4):
                    tsl = slice(tt * 512, (tt + 1) * 512)
                    b = nb()
                    P.mm(PS[:, b, :], ONESB[:], SQB[:, tsl])
                    P.ts("dve", TMPA[:], PS[:, b, :], 1.0 / 128, NORM_EPS, op0=ALU.mult, op1=ALU.add)
                    P.act(TMPA[:], TMPA[:], AF.Sqrt)
                    P.recip(TMPA[:], TMPA[:])
                    b = nb()
                    for kc in range(8):
                        P.mm(PS[:, b, :], wz[:, kc, :], XB[:, kc, tsl], start=(kc == 0), stop=(kc == 7))
                    P.act(TMPB[:], PS[:, b, :], AF.Silu)
                    P.stt("dve", TMPA[:], OT[:, tsl], cst("normg"), TMPA[:], ALU.mult, ALU.mult)
                    P.tt("dve", OGB[:, tsl], TMPA[:], TMPB[:], ALU.mult)
                    for dc in range(8):
                        b = nb()
                        P.mm(PS[:, b, :], wo[:, dc, :], OGB[:, tsl])
                        resid_add(dc, tsl, b, 1.0 / ALPHA)
            drain(layer_norm_gen(0, 2, 1, 1, LN_EPS / (ALPHA * ALPHA)))
            return layer_norm_gen(1024, 2, 1, 1, LN_EPS / (ALPHA * ALPHA))

        def mixer(l, s, bg=None):
            if l == 0:
                return mixer_ab(s, bg)
            return mixer_c(s, bg)

        done = False
        for s in range(nseq):
            for dc in range(8):
                P.dma("sp", X[:, dc, :], xT[s, dc * 128:(dc + 1) * 128, :])
            for dc in range(8):
                P.copy("dve" if dc % 2 else "act", XB[:, dc, :], X[:, dc, :])
            for l in range(DEPTH):
                bg = ffn(l, 1, 0)
                if stop_after == (l, "ffn1"):
                    drain(bg)
                    break
                bg = mixer(l, s, bg)
                if stop_after == (l, "mix"):
                    drain(bg)
                    break
                bg = ffn(l, 2, 2, bg)
                if stop_after == (l, "ffn2"):
                    drain(bg)
                    break
                ple(l, s, bg)
                if stop_after == (l, "ple"):
                    break
            for dc in range(8):
                P.dma("sp", outT[s, dc * 128:(dc + 1) * 128, :], X[:, dc, :], is_output=True)
        P.emit()
    return nc


def _prep(inputs):
    inp = {k: np.asarray(v) for k, v in inputs.items()}
    B = _wlayout(inp)
    cb, ccols = _clayout(inp)
    return inp, B, cb, ccols


def kernel(**inputs):
    inp, B, cb, ccols = _prep(inputs)
    wb = B.cat()
    nc = build_program(SEQ_PER_CORE, B.off, B.n, ccols, cb.shape[1])
    x = inp["x"]
    p = inp["p"]
    in_maps = []
    for c in range(NCORES):
        sl = slice(c * SEQ_PER_CORE, (c + 1) * SEQ_PER_CORE)
        in_maps.append({
            "xT": np.ascontiguousarray(x[sl].transpose(0, 2, 1)),
            "pT": np.ascontiguousarray(p[:, sl].transpose(0, 1, 3, 2)),
            "wblob": wb,
            "cblob": cb,
        })
    res = run_bass_kernel_spmd(nc, in_maps, core_ids=list(range(NCORES)))
    out = np.concatenate([r["outT"] for r in res.results], 0)
    return np.ascontiguousarray(out.transpose(0, 2, 1)).astype(np.float32)
```

```python
import contextlib
import numpy as np
import concourse.bass as bass
import concourse.mybir as mybir
from concourse.bass_utils import run_bass_kernel_spmd

F32 = mybir.dt.float32
BF16 = mybir.dt.bfloat16
ALU = mybir.AluOpType
AF = mybir.ActivationFunctionType
AX = mybir.AxisListType

D = 1024
S = 2048
DFF = 2816
NFC = 22
DPLE = 256
DEPTH = 2
ALPHA = (2.0 * DEPTH) ** 0.25
LN_EPS = 1e-5
NORM_EPS = 1e-6
NCORES = 8
SEQ_PER_CORE = 4

EPOCH = 30000
NDMA = 8


class _Ins:
    __slots__ = ("fn", "waits", "flag", "dma", "cum")

    def __init__(self, fn, dma=None):
        self.fn = fn
        self.waits = []
        self.flag = False
        self.dma = dma
        self.cum = 0


def _box(ap):
    t = ap.tensor
    dims = ap.ap
    off = int(ap.offset)
    sz = mybir.dt.size(ap.dtype)
    sp = str(ap.space)
    if sp in ("SB", "PSUM"):
        rs, pc = dims[0]
        if rs == 0:
            p0 = 0
            f0 = off
        else:
            p0 = off // rs
            f0 = off - p0 * rs
        ext = 1
        for st, cn in dims[1:]:
            ext += (cn - 1) * abs(st)
        return (t.name, p0, p0 + pc, f0 * sz, (f0 + ext) * sz)
    ext = 1
    for st, cn in dims:
        ext += (cn - 1) * abs(st)
    return (t.name, 0, 1, off * sz, (off + ext) * sz)


class Prog:
    ENG = ["pe", "act", "dve", "pool", "sp"]

    def __init__(self, nc):
        self.nc = nc
        self.streams = {e: [] for e in self.ENG}
        self.track = {}
        self.waited_c = {e: {} for e in self.ENG}
        self.waited_d = {e: {} for e in self.ENG}
        self.dma_k = {e: 0 for e in self.ENG}
        self.out_dmas = []

    def _deps(self, eng, reads, writes):
        toks = set()
        for ap, is_w in [(a, False) for a in reads] + [(a, True) for a in writes]:
            name, p0, p1, f0, f1 = _box(ap)
            recs = self.track.get(name)
            if not recs:
                continue
            keep = []
            for r in recs:
                rp0, rp1, rf0, rf1, rtok, rw = r
                ov = not (rp1 <= p0 or p1 <= rp0 or rf1 <= f0 or f1 <= rf0)
                if ov and (rw or is_w):
                    toks.add(rtok)
                if is_w and ov and rp0 >= p0 and rp1 <= p1 and rf0 >= f0 and rf1 <= f1:
                    continue
                keep.append(r)
            self.track[name] = keep
        return toks

    def _register(self, tok, reads, writes):
        for ap, is_w in [(a, False) for a in reads] + [(a, True) for a in writes]:
            name, p0, p1, f0, f1 = _box(ap)
            recs = self.track.setdefault(name, [])
            if not is_w and tok[0] == "c":
                for i, r in enumerate(recs):
                    if (not r[5]) and r[0] == p0 and r[1] == p1 and r[2] == f0 and r[3] == f1 \
                            and r[4][0] == "c" and r[4][1] == tok[1]:
                        recs[i] = (p0, p1, f0, f1, tok, False)
                        break
                else:
                    recs.append((p0, p1, f0, f1, tok, False))
            else:
                recs.append((p0, p1, f0, f1, tok, is_w))

    def _add_waits(self, eng, ins, toks):
        for tok in toks:
            if tok[0] == "c":
                _, f, idx = tok
                if f == eng and eng == "pe":
                    continue
                if self.waited_c[eng].get(f, -1) >= idx:
                    continue
                self.waited_c[eng][f] = idx
                self.streams[f][idx].flag = True
                ins.waits.append(tok)
            else:
                _, q, k = tok
                key = (q, k % NDMA)
                if self.waited_d[eng].get(key, -1) >= k:
                    continue
                self.waited_d[eng][key] = k
                ins.waits.append(tok)

    def op(self, eng, fn, reads=(), writes=()):
        reads = [a for a in reads if a is not None and not isinstance(a, (int, float))]
        writes = [a for a in writes if a is not None]
        ins = _Ins(fn)
        toks = self._deps(eng, reads, writes)
        self._add_waits(eng, ins, toks)
        idx = len(self.streams[eng])
        self.streams[eng].append(ins)
        self._register(("c", eng, idx), reads, writes)
        return ins

    def dma(self, q, out, in_, is_output=False, **kw):
        k = self.dma_k[q]
        self.dma_k[q] += 1
        ins = _Ins(lambda e: e.dma_start(out=out, in_=in_, **kw), dma=(q, k))
        toks = self._deps(q, [in_], [out])
        if k >= NDMA:
            toks.add(("d", q, k - NDMA))
        self._add_waits(q, ins, toks)
        self.streams[q].append(ins)
        tok = ("d", q, k)
        self._register(tok, [in_], [out])
        if is_output:
            self.out_dmas.append(tok)
        return ins

    def mm(self, out, lhsT, rhs, start=True, stop=True):
        return self.op("pe", lambda e: e.matmul(out, lhsT, rhs, start=start, stop=stop),
                       [lhsT, rhs], [out])

    def transpose(self, out, in_, ident):
        return self.op("pe", lambda e: e.transpose(out, in_, ident), [in_, ident], [out])

    def act(self, out, in_, func, bias=0.0, scale=1.0, accum_out=None):
        kw = {}
        if accum_out is not None:
            kw["accum_out"] = accum_out
        return self.op("act", lambda e: e.activation(out, in_, func, bias=bias, scale=scale, **kw),
                       [in_, bias, scale], [out, accum_out])

    def tt(self, eng, out, in0, in1, op):
        return self.op(eng, lambda e: e.tensor_tensor(out, in0, in1, op), [in0, in1], [out])

    def ts(self, eng, out, in0, s1, s2=None, op0=ALU.mult, op1=None):
        kw = {}
        if op1 is not None:
            kw["op1"] = op1
        return self.op(eng, lambda e: e.tensor_scalar(out, in0, s1, s2, op0, **kw),
                       [in0, s1, s2], [out])

    def stt(self, eng, out, in0, scalar, in1, op0, op1):
        return self.op(eng, lambda e: e.scalar_tensor_tensor(out, in0, scalar, in1, op0, op1),
                       [in0, scalar, in1], [out])

    def copy(self, eng, out, in_):
        if eng == "act":
            return self.op("act", lambda e: e.copy(out, in_), [in_], [out])
        return self.op(eng, lambda e: e.tensor_copy(out, in_), [in_], [out])

    def memset(self, eng, out, val):
        return self.op(eng, lambda e: e.memset(out, val), [], [out])

    def reduce(self, eng, out, in_, op, axis=AX.X):
        return self.op(eng, lambda e: e.tensor_reduce(out, in_, axis, op), [in_], [out])

    def recip(self, out, in_):
        return self.op("dve", lambda e: e.reciprocal(out, in_), [in_], [out])

    def scan(self, out, d0, d1, init, op0, op1):
        return self.op("dve", lambda e: e.tensor_tensor_scan(out, d0, d1, init, op0, op1),
                       [d0, d1, init], [out])

    def emit(self):
        nc = self.nc
        fin = _Ins(lambda e: None)
        self._add_waits("sp", fin, set(self.out_dmas))
        self.streams["sp"].append(fin)
        nsem = {}
        for e in self.ENG:
            cum = 0
            for ins in self.streams[e]:
                if ins.flag:
                    cum += 1
                ins.cum = cum
            nsem[e] = (cum + EPOCH - 1) // EPOCH
        with contextlib.ExitStack() as es:
            csem = {e: [es.enter_context(nc.semaphore(f"c_{e}_{i}")) for i in range(nsem[e])]
                    for e in self.ENG}
            dsem = {e: [es.enter_context(nc.semaphore(f"d_{e}_{i}")) for i in range(NDMA)]
                    for e in self.ENG if self.dma_k[e] > 0}
            block = es.enter_context(nc.Block())
            streams = self.streams

            def run(e, eh):
                for ins in streams[e]:
                    for tok in ins.waits:
                        if tok[0] == "c":
                            c = streams[tok[1]][tok[2]].cum
                            eh.wait_ge(csem[tok[1]][(c - 1) // EPOCH], (c - 1) % EPOCH + 1)
                        else:
                            _, q, k = tok
                            eh.wait_ge(dsem[q][k % NDMA], 16 * (k // NDMA + 1))
                    r = ins.fn(eh)
                    if r is None:
                        continue
                    if ins.dma is not None:
                        q, k = ins.dma
                        r.then_inc(dsem[q][k % NDMA], 16)
                    elif ins.flag:
                        c = ins.cum
                        r.then_inc(csem[e][(c - 1) // EPOCH], 1)

            @block.tensor
            def _(eh):
                run("pe", eh)

            @block.scalar
            def _(eh):
                run("act", eh)

            @block.vector
            def _(eh):
                run("dve", eh)

            @block.gpsimd
            def _(eh):
                run("pool", eh)

            @block.sync
            def _(eh):
                run("sp", eh)


class _Blob:
    def __init__(self):
        self.parts = []
        self.off = {}
        self.n = 0

    def add(self, name, arr):
        a = np.ascontiguousarray(arr, dtype=np.float32).reshape(-1)
        assert a.size % 128 == 0, name
        self.off[name] = (self.n, a.size)
        self.parts.append(a)
        self.n += a.size

    def cat(self):
        return np.concatenate(self.parts)


def _wlayout(inp):
    B = _Blob()
    for l in range(DEPTH):
        for f in (1, 2):
            wg = inp[f"ffn{f}_wg"][l].reshape(8, 128, NFC, 128)
            wu = inp[f"ffn{f}_wu"][l].reshape(8, 128, NFC, 128)
            gu = np.stack([wg, wu], 0)
            B.add(f"gu{l}{f}", gu.transpose(3, 2, 0, 1, 4))
            wd = inp[f"ffn{f}_wd"][l].reshape(NFC, 128, 8, 128)
            B.add(f"wd{l}{f}", wd.transpose(2, 1, 0, 3))
        pg = inp["ple_wg"][l].reshape(8, 128, 8, 128)
        pp = inp["ple_wp"][l].reshape(2, 128, 8, 128)
        B.add(f"ple{l}", np.concatenate([pg, pp], 0).transpose(2, 1, 0, 3))
    wi = inp["ab_w_in"][0].reshape(8, 128, 1792)
    B.add("abq", wi[:, :, 0:512].reshape(8, 128, 8, 64).transpose(2, 1, 0, 3))
    B.add("abk", wi[:, :, 512:640].reshape(8, 128, 2, 64).transpose(2, 1, 0, 3))
    B.add("abv", wi[:, :, 640:768].transpose(1, 0, 2))
    B.add("abx", wi[:, :, 768:1280].reshape(8, 128, 4, 128).transpose(2, 1, 0, 3))
    B.add("abg", wi[:, :, 1280:1792].reshape(8, 128, 4, 128).transpose(2, 1, 0, 3))
    wo = inp["ab_w_out"][0]
    B.add("aboa", wo[:512].reshape(8, 64, 8, 128).transpose(2, 1, 0, 3))
    B.add("abob", wo[512:].reshape(4, 128, 8, 128).transpose(2, 1, 0, 3))
    for nm, key in (("abwa", "b_wa"), ("abwx", "b_wx")):
        w = inp[key][0]
        bd = np.zeros((4, 128, 128), np.float32)
        for c in range(4):
            for hb in range(2):
                bd[c, hb * 64:(hb + 1) * 64, hb * 64:(hb + 1) * 64] = w[2 * c + hb]
        B.add(nm, bd)
    ci = inp["c_w_in"][0].reshape(8, 128, 4112)
    B.add("cw", ci[:, :, :4096].reshape(8, 128, 4, 8, 128).transpose(3, 2, 1, 0, 4))
    B.add("cba", ci[:, :, 4096:4112].transpose(1, 0, 2))
    B.add("cwo", inp["c_w_out"][0].reshape(8, 128, 8, 128))
    return B


def _clayout(inp):
    cols = {}
    parts = []
    n = 0

    def add(name, a):
        nonlocal n
        a = np.ascontiguousarray(a, dtype=np.float32).reshape(128, -1)
        cols[name] = (n, a.shape[1])
        parts.append(a)
        n += a.shape[1]

    add("ln_g", inp["ln_g"].reshape(6, 8, 128).transpose(2, 0, 1))
    add("ln_b", inp["ln_b"].reshape(6, 8, 128).transpose(2, 0, 1))
    add("ple_bg", inp["ple_bg"].reshape(2, 8, 128).transpose(2, 0, 1))
    add("ones", np.ones((128, 128), np.float32))
    add("ident", np.eye(128, dtype=np.float32))
    pidx = np.arange(128)
    sk = inp["a_sinks"][0]
    add("sink", np.stack([sk[2 * hp + pidx // 64] for hp in range(4)], 1))
    slopes = (2.0 ** (-8.0 * np.arange(1, 9, dtype=np.float32) / 8)).astype(np.float32)
    dist = np.abs((pidx % 64)[:, None] + 128 - np.arange(192)[None, :]).astype(np.float32)
    add("abias", np.stack([-(slopes[2 * hp + pidx // 64][:, None] * dist) for hp in range(4)], 1))
    add("convw", inp["b_conv_w"][0].reshape(4, 4, 128).transpose(2, 1, 0))
    add("convb", inp["b_conv_b"][0].reshape(4, 128).T)
    add("bba", inp["b_ba"][0].reshape(4, 128).T)
    add("bbx", inp["b_bx"][0].reshape(4, 128).T)
    add("blam", inp["b_lam"][0].reshape(4, 128).T)
    add("cconv", inp["c_conv_w"][0].reshape(4, 24, 128).transpose(2, 1, 0))
    add("alog", np.broadcast_to(inp["c_a_log"][0][None, :], (128, 8)))
    add("dtb", np.broadcast_to(inp["c_dt_bias"][0][None, :], (128, 8)))
    add("normg", inp["c_norm_g"][0].reshape(128, 1))
    kk = np.arange(128)[:, None]
    ii = np.arange(64)[None, :]
    add("maskU", (kk <= ii).astype(np.float32))
    add("maskneg", np.where(ii >= kk, 0.0, -1e30).astype(np.float32))
    add("nstrict", np.where(ii > kk, -1.0, 0.0).astype(np.float32))
    k2 = np.arange(128)[:, None]
    i2 = np.arange(128)[None, :]
    same = (k2 // 64) == (i2 // 64)
    add("maskUbd", (same & ((k2 % 64) <= (i2 % 64))).astype(np.float32))
    add("masknegbd", np.where(same & ((i2 % 64) >= (k2 % 64)), 0.0, -1e30).astype(np.float32))
    add("nstrictbd", np.where(same & ((i2 % 64) > (k2 % 64)), -1.0, 0.0).astype(np.float32))
    return np.concatenate(parts, 1), cols


def build_program(nseq, woff, wtotal, ccols, nccols, stop_after=None):
    nc = bass.Bass("TRN2", target_bir_lowering=False)
    P = Prog(nc)
    xT = nc.dram_tensor("xT", [nseq, D, S], F32, kind="ExternalInput").ap()
    pT = nc.dram_tensor("pT", [DEPTH, nseq, DPLE, S], F32, kind="ExternalInput").ap()
    wblob = nc.dram_tensor("wblob", [wtotal], F32, kind="ExternalInput").ap()
    cblob = nc.dram_tensor("cblob", [128, nccols], F32, kind="ExternalInput").ap()
    outT = nc.dram_tensor("outT", [nseq, D, S], F32, kind="ExternalOutput").ap()
    wsc = nc.dram_tensor("wsc", [wtotal], BF16, kind="Internal").ap()

    def wview(name, pattern, **kw):
        o, n = woff[name]
        return wsc[o:o + n].rearrange(pattern, **kw)

    with contextlib.ExitStack() as es:
        def sb(name, shape, dt):
            return es.enter_context(nc.sbuf_tensor(name, shape, dt))

        X = sb("X", [128, 8, S], F32)
        XB = sb("XB", [128, 8, S], BF16)
        CST = sb("CST", [128, nccols], F32)
        ONESB = sb("ONESB", [128, 128], BF16)
        WGU = [sb(f"WGU{i}", [128, 2, 8, 128], BF16) for i in range(3)]
        WD = [sb(f"WD{i}", [128, NFC, 128], BF16) for i in range(2)]
        AR_BYTES = 80 * 1024
        ARENA = sb("ARENA", [128, AR_BYTES // 2], BF16)
        PS = es.enter_context(nc.psum_tensor("PS", [128, 8, 512], F32))

        def carve(off_bytes, shape, dt, parts=128):
            n = int(np.prod(shape[1:]))
            szb = 2 if dt == BF16 else 4
            assert off_bytes % 4 == 0 and off_bytes + n * szb <= AR_BYTES, (off_bytes, shape)
            a = ARENA[0:parts, off_bytes // 2: off_bytes // 2 + n * szb // 2]
            if dt == F32:
                a = a.bitcast(F32)
            if len(shape) == 3:
                a = a.rearrange("p (a b) -> p a b", a=shape[1])
            elif len(shape) == 4:
                a = a.rearrange("p (a b c) -> p a b c", a=shape[1], b=shape[2])
            return a

        def cst(name, j=0, n=1):
            o, w = ccols[name]
            return CST[:, o + j:o + j + n]

        P.dma("sp", CST[:], cblob)
        o, w = ccols["ones"]
        P.copy("dve", ONESB[:], CST[:, o:o + 128])
        CH = 1 << 21
        pos = 0
        while pos < wtotal:
            n = min(CH, wtotal - pos)
            P.dma("pool", wsc[pos:pos + n].rearrange("(p f) -> p f", p=128),
                  wblob[pos:pos + n].rearrange("(p f) -> p f", p=128))
            pos += n

        wslot = [0]

        def pump(gen, n=1):
            if gen is None:
                return None
            for _ in range(n):
                try:
                    next(gen)
                except StopIteration:
                    return None
            return gen

        def drain(gen):
            while gen is not None:
                gen = pump(gen, 64)

        def layer_norm_gen(t0, nt, l, i, eps):
            for tt in range(nt):
                ts_ = slice(t0 + tt * 512, t0 + (tt + 1) * 512)
                RB = carve(64 * 1024, [128, 8, 512], BF16)
                RQ = carve(72 * 1024, [128, 8, 512], BF16)
                xs = X[:, :, ts_]
                for hh in range(2):
                    P.act(RB[:, hh * 4:(hh + 1) * 4, :], X[:, hh * 4:(hh + 1) * 4, ts_], AF.Copy)
                    yield
                    P.act(RQ[:, hh * 4:(hh + 1) * 4, :], X[:, hh * 4:(hh + 1) * 4, ts_], AF.Square)
                    yield
                for dc in range(8):
                    P.mm(PS[:, 6, :], ONESB[:], RB[:, dc, :], start=(dc == 0), stop=(dc == 7))
                for dc in range(8):
                    P.mm(PS[:, 7, :], ONESB[:], RQ[:, dc, :], start=(dc == 0), stop=(dc == 7))
                yield
                ST = carve(44 * 1024 + 2048 + (tt % 2) * 6144, [128, 3, 512], F32)
                M, V, R = ST[:, 0, :], ST[:, 1, :], ST[:, 2, :]
                P.ts("dve", M, PS[:, 6, :], 1.0 / D, None, op0=ALU.mult)
                P.tt("dve", V, M, M, ALU.mult)
                P.stt("dve", V, PS[:, 7, :], 1.0 / D, V, ALU.mult, ALU.subtract)
                P.ts("dve", V, V, eps, None, op0=ALU.add)
                P.act(R, V, AF.Sqrt)
                P.recip(R, R)
                yield
                for hh in range(2):
                    xh = X[:, hh * 4:(hh + 1) * 4, ts_]
                    P.tt("dve", xh, xh, M.unsqueeze(1).to_broadcast([128, 4, 512]), ALU.subtract)
                    yield
                    P.tt("dve", xh, xh, R.unsqueeze(1).to_broadcast([128, 4, 512]), ALU.mult)
                    yield
                for dc in range(8):
                    xd = X[:, dc, ts_]
                    gcol = cst("ln_g", (l * 3 + i) * 8 + dc)
                    bcol = cst("ln_b", (l * 3 + i) * 8 + dc)
                    P.act(XB[:, dc, ts_], xd, AF.Identity, bias=bcol, scale=gcol)
                    P.act(xd, xd, AF.Identity, bias=bcol, scale=gcol)
                    if dc % 2:
                        yield

        def layer_norm(t0, nt, l, i, eps):
            drain(layer_norm_gen(t0, nt, l, i, eps))

        def ffn(l, f, lni, bg=None):
            gu = wview(f"gu{l}{f}", "(fc p r) -> fc p r", fc=NFC, p=128)
            wd = wview(f"wd{l}{f}", "(dc p r) -> dc p r", dc=8, p=128)
            H = carve(0, [128, NFC, 1024], BF16)
            SG = carve(44 * 1024, [128, 2, 512], BF16)
            for half in range(2):
                t0 = half * 1024
                for fc in range(NFC):
                    wb = WGU[wslot[0] % 3]
                    wslot[0] += 1
                    P.dma("sp", wb[:].rearrange("p a b c -> p (a b c)"), gu[fc])
                    for sub in range(2):
                        tsl = slice(t0 + sub * 512, t0 + (sub + 1) * 512)
                        bg_, bu = (0, 1) if (fc * 2 + sub) % 2 == 0 else (2, 3)
                        for kc in range(8):
                            P.mm(PS[:, bg_, :], wb[:, 0, kc, :], XB[:, kc, tsl], start=(kc == 0), stop=(kc == 7))
                        for kc in range(8):
                            P.mm(PS[:, bu, :], wb[:, 1, kc, :], XB[:, kc, tsl], start=(kc == 0), stop=(kc == 7))
                        sg = SG[:, (fc * 2 + sub) % 2, :]
                        P.act(sg, PS[:, bg_, :], AF.Silu)
                        P.tt("dve", H[:, fc, sub * 512:(sub + 1) * 512], sg, PS[:, bu, :], ALU.mult)
                        bg = pump(bg, 2)
                drain(bg)
                for dc in range(8):
                    wdb = WD[dc % 2]
                    P.dma("sp", wdb[:].rearrange("p a b -> p (a b)"), wd[dc])
                    for sub in range(2):
                        tsl = slice(t0 + sub * 512, t0 + (sub + 1) * 512)
                        bank = 4 + (dc * 2 + sub) % 2
                        for fc in range(NFC):
                            P.mm(PS[:, bank, :], wdb[:, fc, :], H[:, fc, sub * 512:(sub + 1) * 512],
                                 start=(fc == 0), stop=(fc == NFC - 1))
                        xs = X[:, dc, tsl]
                        P.stt("dve", xs, PS[:, bank, :], 0.5 / ALPHA, xs, ALU.mult, ALU.add)
                bg = layer_norm_gen(t0, 2, l, lni, LN_EPS / (ALPHA * ALPHA))
            return bg

        def ple(l, s, bg=None):
            pw = wview(f"ple{l}", "(dc p r) -> p dc r", dc=8, p=128)
            PT = carve(0, [128, 2, S], BF16)
            for kc in range(2):
                P.dma("pool", PT[:, kc, :], pT[l, s, kc * 128:(kc + 1) * 128, :])
            SGM = carve(8 * 1024, [128, 2, 512], F32)
            WP = carve(12 * 1024, [128, 8, 10, 128], BF16)
            for dc in range(8):
                P.dma("sp", WP[:, dc].rearrange("p a b -> p (a b)"), pw[:, dc, :])
            for tt in range(4):
                tsl = slice(tt * 512, (tt + 1) * 512)
                if tt == 2:
                    drain(bg)
                    bg = None
                for dc in range(8):
                    wb = WP[:, dc]
                    i2 = (dc * 4 + tt) % 2
                    for kc in range(8):
                        P.mm(PS[:, i2, :], wb[:, kc, :], XB[:, kc, tsl], start=(kc == 0), stop=(kc == 7))
                    for kc in range(2):
                        P.mm(PS[:, 2 + i2, :], wb[:, 8 + kc, :], PT[:, kc, tsl], start=(kc == 0), stop=(kc == 1))
                    sg = SGM[:, i2, :]
                    P.act(sg, PS[:, i2, :], AF.Sigmoid, bias=cst("ple_bg", l * 8 + dc))
                    P.tt("dve", sg, sg, PS[:, 2 + i2, :], ALU.mult)
                    P.tt("dve", X[:, dc, tsl], X[:, dc, tsl], sg, ALU.add)
                    bg = pump(bg, 2)
                for hh in range(2):
                    P.copy("act" if hh else "dve", XB[:, hh * 4:(hh + 1) * 4, tsl], X[:, hh * 4:(hh + 1) * 4, tsl])

        IDB = sb("IDB", [128, 128], BF16)
        o_, w_ = ccols["ident"]
        P.copy("dve", IDB[:], CST[:, o_:o_ + 128])
        C0 = sb("C0", [128, 8], F32)
        P.act(C0[:, 0:4], cst("blam", 0, 4), AF.Exp, scale=-1.0)
        P.act(C0[:, 0:4], C0[:, 0:4], AF.Ln, bias=1.0)
        P.ts("dve", C0[:, 4:8], C0[:, 0:4], -16.0, None, op0=ALU.mult)
        P.ts("dve", C0[:, 0:4], C0[:, 0:4], -8.0, None, op0=ALU.mult)

        def wload(view, shape_free):
            wb = WGU[wslot[0] % 3]
            wslot[0] += 1
            n = int(np.prod(shape_free))
            flat = wb[:].rearrange("p a b c -> p (a b c)")
            np_ = view.shape[0]
            P.dma("sp", flat[0:np_, 0:n], view)
            return flat[0:np_, 0:n]

        def resid_add(dc, tsl, bank, scale):
            xs = X[:, dc, tsl]
            P.stt("dve", xs, PS[:, bank, :], scale, xs, ALU.mult, ALU.add)

        def mixer_ab(s, bg=None):
            bankc = [0]

            def nb(lo=0, hi=6):
                b = lo + bankc[0] % (hi - lo)
                bankc[0] += 1
                return b

            KT = carve(0, [64, 2, S], BF16, parts=64)
            V = carve(8 * 1024, [64, 32, 128], BF16, parts=64)
            QT = carve(16 * 1024, [64, 16 * 512], BF16, parts=64)
            QTv = QT.rearrange("p (n hp h2 c) -> p n hp h2 c", n=16, hp=4, h2=2)
            YAT = carve(32 * 1024, [64, 8, 1024], BF16, parts=64)
            WK0 = 48 * 1024

            def bc_last(ap, n):
                return ap.unsqueeze(len(ap.shape)).to_broadcast(list(ap.shape) + [n])

            wk = wview("abk", "(kv p r) -> kv p r", kv=2, p=128)
            wks = [wload(wk[kv], [8, 64]).rearrange("p (a b) -> p a b", a=8) for kv in range(2)]
            wv = wload(wview("abv", "(p r) -> p r", p=128), [8, 128]).rearrange("p (a b) -> p a b", a=8)
            for tt in range(4):
                if tt == 2:
                    drain(bg)
                    bg = None
                for kv in range(2):
                    tsl = slice(tt * 512, (tt + 1) * 512)
                    b = nb(0, 4)
                    for kc in range(8):
                        P.mm(PS[0:64, b, :], wks[kv][:, kc, :], XB[:, kc, tsl], start=(kc == 0), stop=(kc == 7))
                    P.copy("act", KT[:, kv, tsl], PS[0:64, b, :])
                    bg = pump(bg, 6)
            for g in range(8):
                if g < 4:
                    bg = pump(bg, 6)
                else:
                    drain(bg)
                    bg = None
                b = nb(0, 4)
                for cc in range(4):
                    n = g * 4 + cc
                    for kc in range(8):
                        P.mm(PS[0:64, b, cc * 128:(cc + 1) * 128], XB[:, kc, n * 64:(n + 1) * 64], wv[:, kc, :],
                             start=(kc == 0), stop=(kc == 7))
                P.copy("act", V[:, g * 4:(g + 1) * 4, :], PS[0:64, b, :].rearrange("p (a b) -> p a b", a=4))
            wq = wview("abq", "(h p r) -> h p r", h=8, p=128)
            woa = wview("aboa", "(dc p r) -> dc p r", dc=8, p=64)
            PSB = PS[:, 4:6, :].bitcast(BF16).rearrange("p a b -> p (a b)")
            o_, w_ = ccols["abias"]
            ABIAS = CST[:, o_:o_ + 768].rearrange("p (h k) -> p h k", h=4)
            o_, w_ = ccols["sink"]
            SINK4 = CST[:, o_:o_ + 4]
            NSINK4 = carve(WK0 + 15360 + 128, [128, 4], F32)
            P.ts("dve", NSINK4[:], SINK4, -1.0, None, op0=ALU.mult)
            for half in range(2):
                T0 = half * 1024
                for h in range(8):
                    w = wload(wq[h], [8, 64]).rearrange("p (a b) -> p a b", a=8)
                    for t2 in range(2):
                        tsl = slice(T0 + t2 * 512, T0 + (t2 + 1) * 512)
                        b = nb(0, 4)
                        for kc in range(8):
                            P.mm(PS[0:64, b, :], w[:, kc, :], XB[:, kc, tsl], start=(kc == 0), stop=(kc == 7))
                        P.act(QTv[:, t2 * 8:(t2 + 1) * 8, h // 2, h % 2, :],
                              PS[0:64, b, :].rearrange("p (n c) -> p n c", n=8), AF.Copy, scale=0.125)
                def geom(nl):
                    n = half * 16 + nl
                    j0 = max(0, 2 - n)
                    nj = 3 - j0
                    i2 = n % 2
                    SBf = carve(WK0 + i2 * 3072, [128, 2, 2, 192], F32)
                    PN = carve(WK0 + 6144 + i2 * 1536, [128, 4, 192], BF16)
                    PTs = carve(WK0 + 9216 + i2 * 3072, [64, 4, 3, 128], BF16, parts=64)
                    STt = carve(WK0 + 15360 + i2 * 64, [128, 16], F32)
                    return n, j0, nj, nj * 64, (n - 2 + j0) * 64, i2, SBf, SBf.rearrange("p b q k -> p (b q) k"), PN, PTs, STt

                def stage_a1(nl):
                    n, j0, nj, nk, k0, i2, SBf, SB4, PN, PTs, STt = geom(nl)
                    for hp in range(4):
                        P.mm(PS[:, 2 * i2 + hp // 2, (hp % 2) * 192:(hp % 2) * 192 + nk],
                             QTv[:, nl, hp].rearrange("p a b -> p (a b)"), KT[:, hp // 2, k0:k0 + nk])
                    S4 = PS[:, 2 * i2:2 * i2 + 2, 0:384].rearrange("p b (q k) -> p b q k", q=2)
                    P.tt("dve", SBf[:, :, :, 0:nk], S4[:, :, :, 0:nk],
                         ABIAS[:, :, j0 * 64:192].rearrange("p (b q) k -> p b q k", b=2), ALU.add)
                    P.reduce("dve", STt[:, 0:4], SB4[:, :, 0:nk], ALU.max)
                    P.stt("dve", STt[:, 4:8], STt[:, 0:4], -1.0, NSINK4[:], ALU.mult, ALU.min)
                    P.memset("dve", STt[:, 8:12], 0.0)
                    P.tt("dve", STt[:, 12:16], SINK4, STt[:, 4:8], ALU.add)
                    for hp in range(4):
                        P.act(SB4[:, hp, 0:nk], SB4[:, hp, 0:nk], AF.Exp, bias=STt[:, 4 + hp:5 + hp],
                              accum_out=STt[:, 8 + hp:9 + hp])
                    P.act(STt[:, 12:16], STt[:, 12:16], AF.Exp)

                def stage_a2(nl):
                    n, j0, nj, nk, k0, i2, SBf, SB4, PN, PTs, STt = geom(nl)
                    P.tt("dve", STt[:, 8:12], STt[:, 8:12], STt[:, 12:16], ALU.add)
                    P.recip(STt[:, 8:12], STt[:, 8:12])
                    P.tt("dve", PN[:, :, 0:nk], SB4[:, :, 0:nk], bc_last(STt[:, 8:12], nk), ALU.mult)

                def stage_b(nl):
                    n, j0, nj, nk, k0, i2, SBf, SB4, PN, PTs, STt = geom(nl)
                    for hp in range(4):
                        for jj in range(nj):
                            idx = hp * 3 + jj
                            P.transpose(PSB[0:64, idx * 128:(idx + 1) * 128], PN[:, hp, jj * 64:(jj + 1) * 64], IDB[:])
                    P.copy("act", PTs[:, :, 0:nj, :],
                           PSB[0:64, 0:1536].rearrange("p (h j c) -> p h j c", h=4, j=3)[:, :, 0:nj, :])
                    for hp in range(4):
                        for jj in range(nj):
                            P.mm(PS[0:64, 6 + i2, hp * 128:(hp + 1) * 128],
                                 V[:, n - 2 + j0 + jj, (hp // 2) * 64:(hp // 2 + 1) * 64], PTs[:, hp, jj, :],
                                 start=(jj == 0), stop=(jj == nj - 1))
                    P.copy("act", YAT[:, :, nl * 64:(nl + 1) * 64],
                           PS[0:64, 6 + i2, :].rearrange("p (h c) -> p h c", h=8))

                stage_a1(0)
                for nl in range(16):
                    if nl + 1 < 16:
                        stage_a1(nl + 1)
                    stage_a2(nl)
                    stage_b(nl)
                for dc in range(8):
                    w = wload(woa[dc], [8, 128]).rearrange("p (a b) -> p a b", a=8)
                    for t2 in range(2):
                        tsl = slice(T0 + t2 * 512, T0 + (t2 + 1) * 512)
                        b = nb(0, 4)
                        for h in range(8):
                            P.mm(PS[:, b, :], w[:, h, :], YAT[:, h, t2 * 512:(t2 + 1) * 512], start=(h == 0), stop=(h == 7))
                        resid_add(dc, tsl, b, 1.0 / ALPHA)
            BXP = carve(0, [128, 2052], F32)
            BXC = carve(8208, [128, S], F32)
            RG = carve(16400, [128, S], F32)
            IG = carve(24592, [128, S], F32)
            GG = carve(32784, [128, S], F32)
            AA = carve(40976, [128, S], F32)
            BXB = carve(49168, [128, S], BF16)
            YBT = carve(53264, [128, 4, S], BF16)
            wx = wview("abx", "(c p r) -> c p r", c=4, p=128)
            wg_ = wview("abg", "(c p r) -> c p r", c=4, p=128)
            wa_ = wview("abwa", "(c p r) -> c p r", c=4, p=128)
            wxx = wview("abwx", "(c p r) -> c p r", c=4, p=128)
            P.memset("pool", BXP[:, 0:3], 0.0)
            for c in range(4):
                w = wload(wx[c], [8, 128]).rearrange("p (a b) -> p a b", a=8)
                for tt in range(4):
                    tsl = slice(tt * 512, (tt + 1) * 512)
                    b = nb()
                    for kc in range(8):
                        P.mm(PS[:, b, :], w[:, kc, :], XB[:, kc, tsl], start=(kc == 0), stop=(kc == 7))
                    P.copy("act", BXP[:, 3 + tt * 512:3 + (tt + 1) * 512], PS[:, b, :])
                w = wload(wg_[c], [8, 128]).rearrange("p (a b) -> p a b", a=8)
                for tt in range(4):
                    tsl = slice(tt * 512, (tt + 1) * 512)
                    b = nb()
                    for kc in range(8):
                        P.mm(PS[:, b, :], w[:, kc, :], XB[:, kc, tsl], start=(kc == 0), stop=(kc == 7))
                    P.act(GG[:, tsl], PS[:, b, :], AF.Gelu)
                o_, w_ = ccols["convw"]
                cw = lambda j: CST[:, o_ + c * 4 + j:o_ + c * 4 + j + 1]
                P.ts("dve", BXC[:], BXP[:, 0:S], cw(0), cst("convb", c), op0=ALU.mult, op1=ALU.add)
                P.stt("dve", BXC[:], BXP[:, 1:1 + S], cw(1), BXC[:], ALU.mult, ALU.add)
                P.stt("dve", BXC[:], BXP[:, 2:2 + S], cw(2), BXC[:], ALU.mult, ALU.add)
                P.stt("dve", BXC[:], BXP[:, 3:3 + S], cw(3), BXC[:], ALU.mult, ALU.add)
                P.copy("act", BXB[:], BXC[:])
                wa = wload(wa_[c], [128])
                wx2 = wload(wxx[c], [128])
                for tt in range(4):
                    tsl = slice(tt * 512, (tt + 1) * 512)
                    b = nb()
                    P.mm(PS[:, b, :], wa, BXB[:, tsl])
                    P.act(RG[:, tsl], PS[:, b, :], AF.Sigmoid, bias=cst("bba", c))
                    b = nb()
                    P.mm(PS[:, b, :], wx2, BXB[:, tsl])
                    P.act(IG[:, tsl], PS[:, b, :], AF.Sigmoid, bias=cst("bbx", c))
                P.act(AA[:], RG[:], AF.Exp, scale=C0[:, c:c + 1])
                P.act(RG[:], RG[:], AF.Exp, scale=C0[:, 4 + c:5 + c])
                P.act(RG[:], RG[:], AF.Sqrt, bias=1.0, scale=-1.0)
                P.tt("dve", IG[:], IG[:], BXC[:], ALU.mult)
                P.tt("dve", IG[:], IG[:], RG[:], ALU.mult)
                P.scan(RG[:], AA[:], IG[:], 0.0, ALU.mult, ALU.add)
                P.tt("dve", YBT[:, c, :], RG[:], GG[:], ALU.mult)
            wob = wview("abob", "(dc p r) -> dc p r", dc=8, p=128)
            for dc in range(8):
                w = wload(wob[dc], [4, 128]).rearrange("p (a b) -> p a b", a=4)
                for tt in range(4):
                    tsl = slice(tt * 512, (tt + 1) * 512)
                    b = nb()
                    for c in range(4):
                        P.mm(PS[:, b, :], w[:, c, :], YBT[:, c, tsl], start=(c == 0), stop=(c == 3))
                    resid_add(dc, tsl, b, 1.0 / ALPHA)
            drain(layer_norm_gen(0, 2, 0, 1, LN_EPS / (ALPHA * ALPHA)))
            return layer_norm_gen(1024, 2, 0, 1, LN_EPS / (ALPHA * ALPHA))

        NEGA = sb("NEGA", [128, 8], F32)
        P.act(NEGA[:], cst("alog", 0, 8), AF.Exp)
        P.ts("dve", NEGA[:], NEGA[:], -1.0, None, op0=ALU.mult)

        def mixer_c(s, bg=None):
            bankc = [0]

            def nb(lo=0, hi=6):
                b = lo + bankc[0] % (hi - lo)
                bankc[0] += 1
                return b

            def bc_last(ap, n):
                return ap.unsqueeze(len(ap.shape)).to_broadcast(list(ap.shape) + [n])

            def bc_mid(ap, n):
                return ap.unsqueeze(1).to_broadcast([ap.shape[0], n, ap.shape[1]])

            GBR = carve(0, [128, 16, 16], F32)
            BETA = carve(1024, [128, 16, 8], F32)
            GNEG = carve(1536, [128, 16, 8], F32)
            CIN = carve(4096, [128, 2052], F32)
            DECT = carve(4112, [128, 16, 128], F32)
            QN = carve(12304, [128, S], F32)
            KN = carve(20496, [128, S], F32)
            KGT = carve(28688, [128, 16, 128], BF16)
            NWT = carve(61456, [128, S], BF16)
            EG = carve(75792 + 1664, [128, 16], F32)
            QG = carve(28688 + 4096, [128, S], BF16)
            GCB = carve(36880, [128, 32, 64], F32)
            OT = carve(36880, [128, S], F32)
            KDEC = carve(45072, [128, 16, 128], BF16)
            VTOK = carve(49168, [128, 16, 128], BF16)
            ATT = carve(53264, [128, 16, 128], BF16)
            ZB = carve(57360, [128, 16, 128], BF16)
            GU = carve(61456, [128, 16, 128], F32)
            XI = carve(69648, [128, 4, 128], F32)
            XTI = carve(71696, [128, 4, 128], F32)
            PM = carve(73744, [128, 4, 128], F32)
            TMPA = carve(69648, [128, 512], F32)
            TMPB = carve(71696, [128, 512], F32)
            SM0 = 75792
            GL = carve(SM0, [128, 32], F32)
            GCC = carve(SM0 + 128, [128, 16], F32)
            GH = carve(SM0 + 192, [128, 16], F32)
            BH = carve(SM0 + 256, [128, 16], F32)
            S2 = carve(SM0 + 320, [128, 16], F32)
            SF = carve(SM0 + 384, [128, 128], F32)
            SBb = carve(SM0 + 896, [128, 128], BF16)
            RB = carve(SM0 + 1152, [128, 128], BF16)
            VN = carve(SM0 + 1408, [128, 128], BF16)
            SQB = carve(28688 + 4096, [128, S], BF16)
            OGB = carve(28688, [128, S], BF16)
            VSB = WD[0][:].rearrange("p a b -> p (a b)")[:, 0:S]

            o_, w_ = ccols["ones"]
            ONESF = CST[:, o_:o_ + 128]
            o_, w_ = ccols["ident"]
            IDF = CST[:, o_:o_ + 128]
            o_, w_ = ccols["maskUbd"]
            MU = CST[:, o_:o_ + 128]
            o_, w_ = ccols["masknegbd"]
            MNEG = CST[:, o_:o_ + 128]
            o_, w_ = ccols["nstrictbd"]
            NSTR = CST[:, o_:o_ + 128]

            drain(bg)
            wba = wload(wview("cba", "(p r) -> p r", p=128), [8, 16]).rearrange("p (a b) -> p a b", a=8)
            b = nb()
            for pr in range(16):
                for kc in range(8):
                    P.mm(PS[:, b, pr * 16:(pr + 1) * 16], XB[:, kc, pr * 128:(pr + 1) * 128], wba[:, kc, :],
                         start=(kc == 0), stop=(kc == 7))
            P.copy("act", GBR[:], PS[:, b, 0:256].rearrange("p (a b) -> p a b", a=16))
            P.act(BETA[:], GBR[:, :, 0:8], AF.Sigmoid)
            o_, w_ = ccols["dtb"]
            P.tt("dve", GNEG[:], GBR[:, :, 8:16], bc_mid(CST[:, o_:o_ + 8], 16), ALU.add)
            P.act(GNEG[:], GNEG[:], AF.Exp)
            P.act(GNEG[:], GNEG[:], AF.Ln, bias=1.0)
            P.tt("dve", GNEG[:], GNEG[:], bc_mid(NEGA[:, :], 16), ALU.mult)

            cwv = wview("cw", "(h g p r) -> h g p r", h=8, g=4, p=128)
            cwo = wview("cwo", "(h p r) -> h p r", h=8, p=128)
            oc_, w_ = ccols["cconv"]

            CINB = carve(4096, [128, 2052], BF16)
            DG = carve(4096 + 4104, [128, 4, 128], BF16)
            o_, w_ = ccols["ident"]
            IDFc = CST[:, o_:o_ + 128]

            def phaseA(h):
                P.memset("pool", CINB[:, 0:4], 0.0)
                for g, dest in ((0, QN), (1, KN), (2, None)):
                    w = wload(cwv[h, g], [8, 128]).rearrange("p (a b) -> p a b", a=8)
                    cw = lambda j: CST[:, oc_ + (g * 8 + h) * 4 + j: oc_ + (g * 8 + h) * 4 + j + 1]
                    for j in range(4):
                        P.ts("pool", DG[:, j, :], IDFc, cw(j), None, op0=ALU.mult)
                    for tt in range(4):
                        tsl = slice(tt * 512, (tt + 1) * 512)
                        b = nb()
                        for kc in range(8):
                            P.mm(PS[:, b, :], w[:, kc, :], XB[:, kc, tsl], start=(kc == 0), stop=(kc == 7))
                        P.copy("act", CINB[:, 3 + tt * 512:3 + (tt + 1) * 512], PS[:, b, :])
                        yield
                    SQ = carve(69648, [128, S], BF16)
                    for tt in range(4):
                        tsl = slice(tt * 512, (tt + 1) * 512)
                        b = nb()
                        for j in range(4):
                            P.mm(PS[:, b, :], DG[:, j, :], CINB[:, j + tt * 512:j + tt * 512 + 512],
                                 start=(j == 0), stop=(j == 3))
                        if g < 2:
                            P.act(dest[:, tsl], PS[:, b, :], AF.Silu)
                            P.act(SQ[:, tsl], dest[:, tsl], AF.Square)
                        else:
                            P.act(VSB[:, tsl], PS[:, b, :], AF.Silu)
                        yield
                    if g < 2:
                        for tt in range(4):
                            tsl = slice(tt * 512, (tt + 1) * 512)
                            b = nb()
                            P.mm(PS[:, b, :], ONESB[:], SQ[:, tsl])
                            rn = carve(73744, [128, 512], F32)
                            sc = 128.0 if g == 0 else 1.0
                            P.act(rn, PS[:, b, :], AF.Ln, bias=NORM_EPS * sc, scale=sc)
                            P.act(rn, rn, AF.Exp, scale=-0.5)
                            P.tt("pool", dest[:, tsl], dest[:, tsl], rn, ALU.mult)
                            yield

            gen = phaseA(0)
            drain(gen)
            for h in range(8):
                P.copy("pool", GH[:], GNEG[:, :, h])
                P.copy("pool", BH[:], BETA[:, :, h])
                P.tt("dve", GU[:], bc_mid(MU, 16), bc_last(GH[:], 128), ALU.mult)
                for q4 in range(4):
                    b = nb()
                    P.mm(PS[:, b, :], ONESF, GU[:, q4 * 4:(q4 + 1) * 4, :].rearrange("p a b -> p (a b)"))
                    P.copy("act", GCB[:, q4 * 8:(q4 + 1) * 8, :].rearrange("p a b -> p (a b)"), PS[:, b, :])
                b = nb()
                P.mm(PS[:, b, 0:16], MU, GH[:])
                P.copy("act", GCC[:], PS[:, b, 0:16])
                P.act(GL[:], GCB[:, :, 63], AF.Exp)
                GCBv = GCB[:].rearrange("p (a b) c -> p a b c", b=2)
                for par in range(2):
                    rows = slice(par * 64, (par + 1) * 64)
                    P.tt("dve", S2[rows, :], GCBv[rows, :, par, 63], GCC[rows, :], ALU.subtract)
                P.act(S2[:], S2[:], AF.Exp)
                P.act(EG[:], GCC[:], AF.Exp)
                for g4 in range(4):
                    b = nb()
                    for cc in range(4):
                        pr = g4 * 4 + cc
                        P.transpose(PS[:, b, cc * 128:(cc + 1) * 128], KN[:, pr * 128:(pr + 1) * 128], IDF)
                    P.tt("dve", KDEC[:, g4 * 4:(g4 + 1) * 4, :], PS[:, b, :].rearrange("p (a b) -> p a b", a=4),
                         bc_last(S2[:, g4 * 4:(g4 + 1) * 4], 128), ALU.mult)
                    P.tt("dve", KGT[:, g4 * 4:(g4 + 1) * 4, :], PS[:, b, :].rearrange("p (a b) -> p a b", a=4),
                         bc_last(EG[:, g4 * 4:(g4 + 1) * 4], 128), ALU.mult)
                    b = nb()
                    PSv = PS[:, b, :].bitcast(BF16)
                    for cc in range(4):
                        pr = g4 * 4 + cc
                        P.transpose(PSv[:, cc * 128:(cc + 1) * 128], VSB[:, pr * 128:(pr + 1) * 128], IDB[:])
                    P.copy("act", VTOK[:, g4 * 4:(g4 + 1) * 4, :], PSv[:, 0:512].rearrange("p (a b) -> p a b", a=4))
                GCBp = GCB[:].rearrange("p (a b) c -> p a (b c)", b=2)
                P.tt("dve", DECT[:], GCBp, bc_last(GCC[:], 128), ALU.subtract)
                P.tt("dve", DECT[:], DECT[:], bc_mid(MNEG, 16), ALU.add)
                P.act(DECT[:], DECT[:], AF.Exp)
                GCBf = GCB[:].rearrange("p a b -> p (a b)")
                P.act(GCBf, GCBf, AF.Exp)
                P.tt("dve", QG[:], QN[:], GCBf, ALU.mult)
                for q4 in range(4):
                    b = nb()
                    for cc in range(4):
                        pr = q4 * 4 + cc
                        P.mm(PS[:, b, cc * 128:(cc + 1) * 128], KN[:, pr * 128:(pr + 1) * 128], QN[:, pr * 128:(pr + 1) * 128])
                    P.tt("dve", ATT[:, q4 * 4:(q4 + 1) * 4, :], PS[:, b, :].rearrange("p (a b) -> p a b", a=4),
                         DECT[:, q4 * 4:(q4 + 1) * 4, :], ALU.mult)
                P.tt("dve", DECT[:], DECT[:], bc_mid(NSTR, 16), ALU.mult)
                P.tt("dve", DECT[:], DECT[:], bc_last(BH[:], 128), ALU.mult)
                bufs = [(XI, XTI, PM),
                        (carve(61456, [128, 4, 128], F32), carve(63504, [128, 4, 128], F32), carve(65552, [128, 4, 128], F32))]
                v4 = lambda b: PS[:, b, :].rearrange("p (a b) -> p a b", a=4)
                for qq in range(2):
                    grp = [(2 * qq + i, bufs[i]) for i in range(2)]
                    for q4, (xi, xti, pm) in grp:
                        b = nb()
                        for cc in range(4):
                            pr = q4 * 4 + cc
                            P.mm(PS[:, b, cc * 128:(cc + 1) * 128], KN[:, pr * 128:(pr + 1) * 128], KN[:, pr * 128:(pr + 1) * 128])
                        P.tt("dve", xi[:], v4(b), DECT[:, q4 * 4:(q4 + 1) * 4, :], ALU.mult)
                    for q4, (xi, xti, pm) in grp:
                        b = nb()
                        for cc in range(4):
                            P.transpose(PS[:, b, cc * 128:(cc + 1) * 128], xi[:, cc, :], IDF)
                        P.copy("act", xti[:], v4(b))
                        P.tt("dve", pm[:], xi[:], bc_mid(IDF, 4), ALU.add)
                    for it in range(5):
                        bb = {}
                        for q4, (xi, xti, pm) in grp:
                            b1 = nb()
                            for cc in range(4):
                                P.mm(PS[:, b1, cc * 128:(cc + 1) * 128], xti[:, cc, :], xi[:, cc, :])
                            b2 = nb()
                            for cc in range(4):
                                P.mm(PS[:, b2, cc * 128:(cc + 1) * 128], xi[:, cc, :], xti[:, cc, :])
                            bb[q4] = (b1, b2)
                        for q4, (xi, xti, pm) in grp:
                            b1, b2 = bb[q4]
                            P.copy("act", xi[:], v4(b1))
                            P.copy("dve", xti[:], v4(b2))
                        for q4, (xi, xti, pm) in grp:
                            b3 = nb()
                            for cc in range(4):
                                P.mm(PS[:, b3, cc * 128:(cc + 1) * 128], xti[:, cc, :], pm[:, cc, :])
                            bb[q4] = b3
                        for q4, (xi, xti, pm) in grp:
                            P.tt("dve", pm[:], pm[:], v4(bb[q4]), ALU.add)
                    for q4, (xi, xti, pm) in grp:
                        P.copy("act", ZB[:, q4 * 4:(q4 + 1) * 4, :], pm[:])
                for g4 in range(4):
                    b = nb()
                    for cc in range(4):
                        pr = g4 * 4 + cc
                        P.mm(PS[:, b, cc * 128:(cc + 1) * 128], KGT[:, pr, :], ZB[:, pr, :])
                    P.act(NWT[:, g4 * 512:(g4 + 1) * 512], PS[:, b, :], AF.Copy, scale=-1.0)
                P.memset("pool", SF[:], 0.0)
                P.memset("pool", SBb[:], 0.0)
                gen = phaseA(h + 1) if h < 7 else None
                for c in range(32):
                    gen = pump(gen, 2)
                    cc = c % 8
                    pr = c // 2
                    pb = (c % 2) * 64
                    rows = slice(pb, pb + 64)
                    csl = slice(c * 64, (c + 1) * 64)
                    ob = 6 + (c // 8) % 2
                    b1 = nb()
                    P.mm(PS[rows, b1, 0:128], ZB[rows, pr, pb:pb + 64], VTOK[rows, pr, :], start=True, stop=False)
                    P.mm(PS[rows, b1, 0:128], NWT[:, csl], SBb[:], start=False, stop=True)
                    P.ts("dve", VN[rows, :], PS[rows, b1, 0:128], BH[rows, pr:pr + 1], None, op0=ALU.mult)
                    P.mm(PS[:, ob, cc * 64:(cc + 1) * 64], SBb[:], QG[:, csl], start=True, stop=False)
                    P.mm(PS[:, ob, cc * 64:(cc + 1) * 64], VN[rows, :], ATT[rows, pr, pb:pb + 64], start=False, stop=True)
                    b3 = nb()
                    P.mm(PS[:, b3, 0:128], KDEC[rows, pr, :], VN[rows, :])
                    P.stt("dve", SBb[:], SF[:], GL[:, c:c + 1], PS[:, b3, 0:128], ALU.mult, ALU.add)
                    P.stt("dve", SF[:], SF[:], GL[:, c:c + 1], PS[:, b3, 0:128], ALU.mult, ALU.add)
                    if cc == 7:
                        P.copy("act", OT[:, (c - 7) * 64:(c + 1) * 64], PS[:, ob, :])
                drain(gen)
                P.act(SQB[:], OT[:], AF.Square)
                wz = wload(cwv[h, 3], [8, 128]).rearrange("p (a b) -> p a b", a=8)
                OGE = WD[1][:].rearrange("p a b -> p (a b)")[:, 0:S]
                OGH = OGE if h % 2 == 0 else OGB
                if h % 2 == 1:
                    woe = wload(cwo[h - 1], [8, 128]).rearrange("p (a b) -> p a b", a=8)
                    wo = wload(cwo[h], [8, 128]).rearrange("p (a b) -> p a b", a=8)
                for tt in range(4):
                    tsl = slice(tt * 512, (tt + 1) * 512)
                    b = nb()
                    P.mm(PS[:, b, :], ONESB[:], SQB[:, tsl])
                    P.ts("dve", TMPA[:], PS[:, b, :], 1.0 / 128, NORM_EPS, op0=ALU.mult, op1=ALU.add)
                    P.act(TMPA[:], TMPA[:], AF.Sqrt)
                    P.recip(TMPA[:], TMPA[:])
                    b = nb()
                    for kc in range(8):
                        P.mm(PS[:, b, :], wz[:, kc, :], XB[:, kc, tsl], start=(kc == 0), stop=(kc == 7))
                    P.act(TMPB[:], PS[:, b, :], AF.Silu)
                    P.stt("dve", TMPA[:], OT[:, tsl], cst("normg"), TMPA[:], ALU.mult, ALU.mult)
                    P.tt("dve", OGH[:, tsl], TMPA[:], TMPB[:], ALU.mult)
                    if h % 2 == 1:
                        for dc in range(8):
                            b = nb()
                            P.mm(PS[:, b, :], woe[:, dc, :], OGE[:, tsl], start=True, stop=False)
                            P.mm(PS[:, b, :], wo[:, dc, :], OGB[:, tsl], start=False, stop=True)
                            resid_add(dc, tsl, b, 1.0 / ALPHA)
            drain(layer_norm_gen(0, 2, 1, 1, LN_EPS / (ALPHA * ALPHA)))
            return layer_norm_gen(1024, 2, 1, 1, LN_EPS / (ALPHA * ALPHA))

        def mixer(l, s, bg=None):
            if l == 0:
                return mixer_ab(s, bg)
            return mixer_c(s, bg)

        done = False
        for s in range(nseq):
            for dc in range(8):
                P.dma("sp", X[:, dc, :], xT[s, dc * 128:(dc + 1) * 128, :])
            for dc in range(8):
                P.copy("dve" if dc % 2 else "act", XB[:, dc, :], X[:, dc, :])
            for l in range(DEPTH):
                bg = ffn(l, 1, 0)
                if stop_after == (l, "ffn1"):
                    drain(bg)
                    break
                bg = mixer(l, s, bg)
                if stop_after == (l, "mix"):
                    drain(bg)
                    break
                bg = ffn(l, 2, 2, bg)
                if stop_after == (l, "ffn2"):
                    drain(bg)
                    break
                ple(l, s, bg)
                if stop_after == (l, "ple"):
                    break
            for dc in range(8):
                P.dma("sp", outT[s, dc * 128:(dc + 1) * 128, :], X[:, dc, :], is_output=True)
        P.emit()
    return nc


def _prep(inputs):
    inp = {k: np.asarray(v) for k, v in inputs.items()}
    B = _wlayout(inp)
    cb, ccols = _clayout(inp)
    return inp, B, cb, ccols


def kernel(**inputs):
    inp, B, cb, ccols = _prep(inputs)
    wb = B.cat()
    nc = build_program(SEQ_PER_CORE, B.off, B.n, ccols, cb.shape[1])
    x = inp["x"]
    p = inp["p"]
    in_maps = []
    for c in range(NCORES):
        sl = slice(c * SEQ_PER_CORE, (c + 1) * SEQ_PER_CORE)
        in_maps.append({
            "xT": np.ascontiguousarray(x[sl].transpose(0, 2, 1)),
            "pT": np.ascontiguousarray(p[:, sl].transpose(0, 1, 3, 2)),
            "wblob": wb,
            "cblob": cb,
        })
    res = run_bass_kernel_spmd(nc, in_maps, core_ids=list(range(NCORES)))
    out = np.concatenate([r["outT"] for r in res.results], 0)
    return np.ascontiguousarray(out.transpose(0, 2, 1)).astype(np.float32)
```
